# Optimizing a Trainium2 kernel written in Bass

```python
import jax, jax.numpy as jnp
from jax import lax
import numpy as np

D_MODEL = 1024
BATCH = 8
SEQ = 4096
DEPTH = 2
DEC_BATCH = 4
DEC_SEQ = 4096
PAST_LEN = 128

CHUNK = 128
A_HEADS = 4
A_HEAD_DIM = 128
A_WIDTH = A_HEADS * A_HEAD_DIM
B_WINDOWS = (2, 4, 8, 16)
B_GROUPS = len(B_WINDOWS)
B_GROUP_DIM = 128
B_WIDTH = B_GROUPS * B_GROUP_DIM
AB_IN = 2 * A_WIDTH + B_WIDTH
AB_OUT = A_WIDTH + B_WIDTH
C_WIDTH = D_MODEL
CONV_W = 3
D_FF = -(-8 * D_MODEL // (3 * 256)) * 256
N_AB_LAYERS = (DEPTH + 1) // 2
N_C_LAYERS = DEPTH // 2
EPS = 1e-6

kernel_name = "hybrid_gmlp_pool_shortconv_encoder"


def rmsnorm(x, g):
    xf = x.astype(jnp.float32)
    y = xf * lax.rsqrt(jnp.mean(xf * xf, axis=-1, keepdims=True) + EPS)
    return (y * g.astype(jnp.float32)).astype(x.dtype)


def chunk_spatial_gate(u, v, g_v, w_s, b_s):
    bn, s, _ = v.shape
    n_c = s // CHUNK
    vf = v.reshape(bn, n_c, CHUNK, A_HEADS, A_HEAD_DIM).astype(jnp.float32)
    mu = jnp.mean(vf, axis=-1, keepdims=True)
    var = jnp.mean(jnp.square(vf - mu), axis=-1, keepdims=True)
    vn = ((vf - mu) * lax.rsqrt(var + EPS) * g_v.reshape(A_HEADS, A_HEAD_DIM).astype(jnp.float32)).astype(v.dtype)
    mixed = jnp.einsum('hpq,bcqhd->bcphd', w_s, vn) + jnp.transpose(b_s)[:, :, None]
    return u * mixed.reshape(bn, s, A_WIDTH)


def multiscale_pool(z, w_pool, pool_scale):
    s = z.shape[1]
    pos = jnp.arange(s)
    outs = []
    for gi, w in enumerate(B_WINDOWS):
        h = w // 2
        zg = z[..., gi * B_GROUP_DIM:(gi + 1) * B_GROUP_DIM].astype(jnp.float32)
        cs = jnp.cumsum(jnp.pad(zg, ((0, 0), (h + 1, h), (0, 0))), axis=1)
        win_sum = cs[:, w:w + s] - cs[:, 0:s]
        count = (jnp.minimum(pos + h, s) - jnp.maximum(pos - h, 0)).astype(jnp.float32)
        r = (win_sum / count[None, :, None] - zg).astype(z.dtype)
        outs.append(jnp.einsum('bsc,cd->bsd', r, w_pool[gi]))
    return jnp.concatenate(outs, axis=-1) * pool_scale


def mixer_ab(x, w_in, g_v, w_s, b_s, w_pool, pool_scale, w_out):
    hcomb = jnp.einsum('bsd,de->bse', x, w_in)
    u = jax.nn.gelu(hcomb[..., :A_WIDTH])
    v = jax.nn.gelu(hcomb[..., A_WIDTH:2 * A_WIDTH])
    z = hcomb[..., 2 * A_WIDTH:]
    a = chunk_spatial_gate(u, v, g_v, w_s, b_s)
    b = multiscale_pool(z, w_pool, pool_scale)
    return jnp.einsum('bse,ed->bsd', jnp.concatenate([a, b], axis=-1), w_out)


def mixer_c(x, w_in, conv_w, w_out):
    s = x.shape[1]
    hcomb = jnp.einsum('bsd,de->bse', x, w_in)
    gate_b = hcomb[..., :C_WIDTH]
    gate_c = hcomb[..., C_WIDTH:2 * C_WIDTH]
    hv = hcomb[..., 2 * C_WIDTH:]
    t = jnp.pad(gate_c * hv, ((0, 0), (1, 1), (0, 0)))
    conv = t[:, 0:s] * conv_w[0] + t[:, 1:s + 1] * conv_w[1] + t[:, 2:s + 2] * conv_w[2]
    return jnp.einsum('bse,ed->bsd', gate_b * conv, w_out)


def swiglu(x, w_gate, w_up, w_down):
    hg = jnp.einsum('bsd,df->bsf', x, w_gate)
    hu = jnp.einsum('bsd,df->bsf', x, w_up)
    return jnp.einsum('bsf,fd->bsd', jax.nn.silu(hg) * hu, w_down)


def trunk(x, norm_g, ab_w_in, ab_v_norm_g, ab_w_spatial, ab_b_spatial, ab_w_pool,
          ab_pool_scale, ab_w_out, c_w_in, c_conv_w, c_w_out, ffn_w_gate, ffn_w_up, ffn_w_down):
    for layer in range(DEPTH):
        g = norm_g[layer]
        h = rmsnorm(x, g[0])
        i = layer // 2
        if layer % 2 == 0:
            h = mixer_ab(h, ab_w_in[i], ab_v_norm_g[i], ab_w_spatial[i], ab_b_spatial[i],
                         ab_w_pool[i], ab_pool_scale[i], ab_w_out[i])
        else:
            h = mixer_c(h, c_w_in[i], c_conv_w[i], c_w_out[i])
        x = x + rmsnorm(h, g[1])
        h = swiglu(rmsnorm(x, g[2]), ffn_w_gate[layer], ffn_w_up[layer], ffn_w_down[layer])
        x = x + rmsnorm(h, g[3])
    return x


def setup_inputs(seed: int = 0) -> dict:
    key = jax.random.key(seed)
    ks = jax.random.split(key, 17)
    f32 = jnp.float32
    nrm = lambda k, shape, scale: jax.random.normal(k, shape, f32) * scale
    return {
        "x_prompt": jax.random.normal(ks[0], (BATCH, SEQ, D_MODEL), f32),
        "x_sample": jax.random.normal(ks[1], (DEC_BATCH, DEC_SEQ, D_MODEL), f32),
        "norm_g": 1.0 + nrm(ks[2], (DEPTH, 4, D_MODEL), 0.02),
        "ab_w_in": nrm(ks[3], (N_AB_LAYERS, D_MODEL, AB_IN), D_MODEL ** -0.5),
        "ab_v_norm_g": 1.0 + nrm(ks[4], (N_AB_LAYERS, A_WIDTH), 0.02),
        "ab_w_spatial": nrm(ks[5], (N_AB_LAYERS, A_HEADS, CHUNK, CHUNK), CHUNK ** -0.5),
        "ab_b_spatial": 1.0 + nrm(ks[6], (N_AB_LAYERS, A_HEADS, CHUNK), 0.02),
        "ab_w_pool": nrm(ks[7], (N_AB_LAYERS, B_GROUPS, B_GROUP_DIM, B_GROUP_DIM), B_GROUP_DIM ** -0.5),
        "ab_pool_scale": 1.0 + nrm(ks[8], (N_AB_LAYERS, B_WIDTH), 0.02),
        "ab_w_out": nrm(ks[9], (N_AB_LAYERS, AB_OUT, D_MODEL), AB_OUT ** -0.5),
        "c_w_in": nrm(ks[10], (N_C_LAYERS, D_MODEL, 3 * C_WIDTH), D_MODEL ** -0.5),
        "c_conv_w": nrm(ks[11], (N_C_LAYERS, CONV_W, C_WIDTH), CONV_W ** -0.5),
        "c_w_out": nrm(ks[12], (N_C_LAYERS, C_WIDTH, D_MODEL), C_WIDTH ** -0.5),
        "ffn_w_gate": nrm(ks[13], (DEPTH, D_MODEL, D_FF), D_MODEL ** -0.5),
        "ffn_w_up": nrm(ks[14], (DEPTH, D_MODEL, D_FF), D_MODEL ** -0.5),
        "ffn_w_down": nrm(ks[15], (DEPTH, D_FF, D_MODEL), D_FF ** -0.5),
    }


def reference(x_prompt, x_sample, norm_g, ab_w_in, ab_v_norm_g, ab_w_spatial, ab_b_spatial,
              ab_w_pool, ab_pool_scale, ab_w_out, c_w_in, c_conv_w, c_w_out,
              ffn_w_gate, ffn_w_up, ffn_w_down):
    y_prompt = trunk(x_prompt, norm_g, ab_w_in, ab_v_norm_g, ab_w_spatial, ab_b_spatial, ab_w_pool,
                     ab_pool_scale, ab_w_out, c_w_in, c_conv_w, c_w_out, ffn_w_gate, ffn_w_up, ffn_w_down)
    y_sample = trunk(x_sample, norm_g, ab_w_in, ab_v_norm_g, ab_w_spatial, ab_b_spatial, ab_w_pool,
                     ab_pool_scale, ab_w_out, c_w_in, c_conv_w, c_w_out, ffn_w_gate, ffn_w_up, ffn_w_down)
    return (y_prompt, y_sample)
```

```python
import contextlib
import sys as _sys
import numpy as np
import concourse.bass as bass
import concourse.mybir as mybir
from concourse.bass_utils import run_bass_kernel_spmd

F32 = mybir.dt.float32
BF16 = mybir.dt.bfloat16
AF = mybir.ActivationFunctionType
ALU = mybir.AluOpType

D = 1024
SEQ = 4096
DFF = 2816
NKC = 8
NFF = 22
EPS = 1e-6
N_CORES = 8

FULL_TILES = [512] * 8
PART_TILES = [128, 512, 512, 512, 384, 128]
TILE_T = FULL_TILES + PART_TILES
NT_ALL = len(TILE_T)
SEG_START = (0, 8)
SEG_END = (7, 13)

SLAB_COLS = 4096
P1_SLABS = 3 + 2 + 11 + 8 + 8
P2_SLABS = 2 + 11 + 8
NSLAB = P1_SLABS + P2_SLABS
RING = 4


def _slab_used():
    u = [4096] * 5 + [4096] * 11 + [NFF * 128] * 8 + [NKC * 384] * 8
    u += [4096] * 2 + [4096] * 11 + [NFF * 128] * 8
    return u


SLAB_USED = _slab_used()


def _pack_k(w, cols):
    k = w.shape[0]
    sub = w[:, cols].reshape(k // 128, 128, len(cols))
    return np.ascontiguousarray(sub.transpose(1, 0, 2)).reshape(128, -1)


def build_wsrc(inp):
    ws = np.zeros((NSLAB, 128, SLAB_COLS), np.float32)
    ar = np.arange
    s = 0
    w_in = inp["ab_w_in"][0]
    ws[s, :, :4096] = _pack_k(w_in, ar(1024, 1536)); s += 1
    ws[s, :, :4096] = _pack_k(w_in, ar(512, 1024)); s += 1
    ws[s, :, :4096] = _pack_k(w_in, ar(0, 512)); s += 1
    w_o = inp["ab_w_out"][0]
    ws[s, :, :4096] = _pack_k(w_o, ar(0, 512)); s += 1
    ws[s, :, :4096] = _pack_k(w_o, ar(512, 1024)); s += 1

    def ffn(layer, s):
        wg, wu, wd = inp["ffn_w_gate"][layer], inp["ffn_w_up"][layer], inp["ffn_w_down"][layer]
        for j in range(11):
            g = _pack_k(wg, ar(256 * j, 256 * j + 256)).reshape(128, 8, 256)
            u = _pack_k(wu, ar(256 * j, 256 * j + 256)).reshape(128, 8, 256)
            ws[s, :, :4096] = np.concatenate([g, u], axis=2).reshape(128, 4096); s += 1
        for i in range(8):
            ws[s, :, :NFF * 128] = _pack_k(wd, ar(128 * i, 128 * i + 128)); s += 1
        return s

    s = ffn(0, s)
    w_c = inp["c_w_in"][0]
    for c in range(8):
        cols = np.concatenate([ar(c * 128, c * 128 + 128), ar(1024 + c * 128, 1024 + c * 128 + 128),
                               ar(2048 + c * 128, 2048 + c * 128 + 128)])
        ws[s, :, :NKC * 384] = _pack_k(w_c, cols); s += 1
    w_co = inp["c_w_out"][0]
    ws[s, :, :4096] = _pack_k(w_co, ar(0, 512)); s += 1
    ws[s, :, :4096] = _pack_k(w_co, ar(512, 1024)); s += 1
    s = ffn(1, s)
    assert s == NSLAB
    return ws


C_G = 0
C_PSC = 64
C_CW = 68
C_BS = 92
C_GVR = 604
NCST = 1116


def build_cst(inp):
    c = np.zeros((128, NCST), np.float32)
    ng = inp["norm_g"].reshape(8, 8, 128)
    c[:, C_G:C_G + 64] = ng.transpose(2, 0, 1).reshape(128, 64)
    c[:, C_PSC:C_PSC + 4] = inp["ab_pool_scale"][0].reshape(4, 128).T
    c[:, C_CW:C_CW + 24] = inp["c_conv_w"][0].reshape(3, 8, 128).transpose(2, 0, 1).reshape(128, 24)
    c[:, C_BS:C_BS + 512] = inp["ab_b_spatial"][0].reshape(1, 512)
    c[:, C_GVR:C_GVR + 512] = inp["ab_v_norm_g"][0].reshape(1, 512)
    return c


def build_wsp(inp):
    w = np.zeros((128, 1024), np.float32)
    ws = inp["ab_w_spatial"][0]
    w[:, 0:512] = ws.transpose(2, 0, 1).reshape(128, 512)
    wp = inp["ab_w_pool"][0]
    w[:, 512:1024] = wp.transpose(1, 0, 2).reshape(128, 512)
    return w


def core_plan(core):
    p = core // 2
    if core % 2 == 0:
        full_seq, part_seq, base = 3 * p, 3 * p + 1, 0
    else:
        full_seq, part_seq, base = 3 * p + 2, 3 * p + 1, 1920
    tiles = []
    off = 0
    for T in FULL_TILES:
        tiles.append((full_seq, off, T)); off += T
    off = base
    for T in PART_TILES:
        tiles.append((part_seq, off, T)); off += T
    return tiles, base


class Buf:
    __slots__ = ("name", "w", "r")

    def __init__(self, name):
        self.name = name
        self.w = None
        self.r = []


class Prog:
    ENGS = ("pe", "act", "dve", "pool", "sp")

    def __init__(self):
        self.lists = {e: [] for e in self.ENGS}
        self.cnt = {e: 0 for e in self.ENGS}
        self.known = {e: {} for e in self.ENGS}
        self.dcnt = {}
        self.toklog = {}

    def _deps(self, eng, reads, writes):
        need = {}
        for b in reads:
            if b.w is not None:
                k, v = b.w
                need[k] = max(need.get(k, 0), v)
        for b in writes:
            if b.w is not None:
                k, v = b.w
                need[k] = max(need.get(k, 0), v)
            for (k, v) in b.r:
                need[k] = max(need.get(k, 0), v)
        kn = self.known[eng]
        for k, v in need.items():
            if kn.get(k, 0) < v:
                kn[k] = v
                self.lists[eng].append(("wait", k, v))

    def _mark(self, tok, reads, writes):
        for b in reads:
            b.r.append(tok)
        for b in writes:
            b.w = tok
            b.r = []

    def op(self, eng, fn, reads=(), writes=()):
        self._deps(eng, reads, writes)
        self.cnt[eng] += 1
        tok = (eng, self.cnt[eng])
        self.lists[eng].append(("ins", fn, eng, 1))
        self.toklog[tok] = _sys._getframe(1).f_lineno
        self._mark(tok, reads, writes)
        return tok

    def group(self, fns, reads=(), writes=(), per=None):
        self._deps("pe", reads, writes)
        self.cnt["pe"] += 1
        tok = ("pe", self.cnt["pe"])
        n = len(fns)
        allr = list(reads)
        for i, f in enumerate(fns):
            if per is not None:
                self._deps("pe", per[i], ())
                allr += list(per[i])
            last = i == n - 1
            self.lists["pe"].append(("ins", f, "pe" if last else None, 1 if last else 0))
        self.toklog[tok] = _sys._getframe(1).f_lineno
        self._mark(tok, allr, writes)
        return tok

    def dma(self, eng, fn, dsem, reads=(), writes=(), nodeps=False):
        if not nodeps:
            self._deps(eng, reads, writes)
        self.dcnt[dsem] = self.dcnt.get(dsem, 0) + 16
        tok = (dsem, self.dcnt[dsem])
        self.lists[eng].append(("ins", fn, dsem, 16))
        self.toklog[tok] = _sys._getframe(1).f_lineno
        self._mark(tok, reads, writes)
        return tok

    def final_wait(self, eng, key, val):
        self.lists[eng].append(("wait", key, val))


def build_nc(tile_T):
    NT = len(tile_T)
    nc = bass.Bass("TRN2", target_bir_lowering=False)
    xin = [nc.dram_tensor(f"xin{t}", [128, 8 * (tile_T[t] + 16)], F32, kind="ExternalInput").ap() for t in range(NT)]
    yout = [nc.dram_tensor(f"y{t}", [128, 8 * tile_T[t]], F32, kind="ExternalOutput").ap() for t in range(NT)]
    wsrc = nc.dram_tensor("wsrc", [NSLAB, 128, SLAB_COLS], F32, kind="ExternalInput").ap()
    cst_d = nc.dram_tensor("cst", [128, NCST], F32, kind="ExternalInput").ap()
    wsp_d = nc.dram_tensor("wsp", [128, 1024], F32, kind="ExternalInput").ap()
    ice_d = nc.dram_tensor("ice", [128, NT * 64], F32, kind="ExternalInput").ap()
    wbf = nc.dram_tensor("wbf", [NSLAB, 128, SLAB_COLS], BF16).ap()

    P = Prog()
    es = contextlib.ExitStack()

    def sb(name, shape, dt):
        return es.enter_context(nc.sbuf_tensor(name, shape, dt))

    with es:
        XR = sb("XR", [128, 8, 528], F32)
        X = [sb(f"X{s}", [128, 8, 512], F32) for s in range(2)]
        XG = sb("XG", [128, 8, 528], BF16)
        HSB = sb("HSB", [128, 8, 512], F32)
        SQ = [sb(f"SQ{r}", [128, 528], BF16) for r in range(3)]
        SQN = [sb(f"SQN{r}", [128, 528], BF16) for r in range(3)]
        Rt = sb("R", [128, 4, 512], BF16)
        AB = sb("AB", [128, 8, 512], BF16)
        ARENA = sb("ARENA", [128, 27 * 256], F32)
        FT = [sb(f"FT{r}", [128, 512], F32) for r in range(3)]
        TTt = [sb(f"TT{r}", [128, 528], F32) for r in range(2)]
        GC = [sb(f"GC{r}", [128, 512], F32) for r in range(2)]
        GBR = [sb(f"GBR{r}", [128, 512], F32) for r in range(2)]
        CV = [sb(f"CV{r}", [128, 512], F32) for r in range(2)]
        CB = [sb(f"CB{s}", [128, 8, 512], BF16) for s in range(2)]
        CAR = sb("CAR", [128, 8, 4], F32)
        RINGT = [sb(f"RG{i}", [128, SLAB_COLS], BF16) for i in range(RING)]
        CST = sb("CST", [128, NCST], F32)
        WSPB = sb("WSPB", [128, 1024], BF16)
        ICE = [sb(f"ICE{i}", [128, 64], F32) for i in range(2)]
        ONES = sb("ONES", [128, 128], BF16)
        E0 = sb("E0", [128, 2], F32)
        RS1 = sb("RS1", [128, 528], F32)
        R1 = sb("R1", [128, 512], F32)
        R2 = sb("R2", [128, 512], F32)
        R2SQ = sb("R2SQ", [128, 512], F32)
        SQT = sb("SQT", [128, 528], F32)
        RSC = sb("RSC", [128, 4], F32)
        ST = [sb(f"ST{r}", [128, 4, 6], F32) for r in range(2)]
        MV = [sb(f"MV{r}", [128, 4, 2], F32) for r in range(2)]
        SV = [sb(f"SV{r}", [128, 4], F32) for r in range(2)]
        RV = [sb(f"RV{r}", [128, 4], F32) for r in range(2)]
        EDG = sb("EDG", [128, 64], F32)
        PS = es.enter_context(nc.psum_tensor("PS", [128, 8, 512], F32))

        sem_names = ["pe", "act", "dve", "pool", "cv0", "cv1", "cv2", "cv3", "cv4", "cv5", "cv6", "cv7", "xr", "y0", "y1", "cst", "wsp", "ice0", "ice1"] + \
                    [f"rg{i}" for i in range(RING)]
        sems = {n: es.enter_context(nc.semaphore(n)) for n in sem_names}

        Hc = [ARENA[:, k * 256:(k + 1) * 256].bitcast(BF16) for k in range(NFF)]
        Zt = ARENA[:, 0:2112].rearrange("p (g w) -> p g w", g=4)
        Pt = ARENA[:, 2304:2304 + 2112].rearrange("p (g w) -> p g w", g=4)
        Qt = ARENA[:, 4608:4608 + 2112].rearrange("p (g w) -> p g w", g=4)
        WS = WSPB[:, 0:512].rearrange("p (h q) -> p h q", h=4)
        WP = WSPB[:, 512:1024].rearrange("p (g d) -> p g d", g=4)
        VN = [HSB[:, 6 + (r % 2), :].bitcast(BF16)[:, (r // 2) * 512:(r // 2) * 512 + 512] for r in range(4)]

        bXR = [Buf(f"XR{c}") for c in range(8)]
        bX = [[Buf(f"X{s}_{c}") for c in range(8)] for s in range(2)]
        bXG = [Buf(f"XG{c}") for c in range(8)]
        bHSB = [Buf(f"HSB{c}") for c in range(8)]
        bSQ = [Buf(f"SQ{r}") for r in range(3)]
        bSQN = [Buf(f"SQN{r}") for r in range(3)]
        bR = [Buf(f"R{g}") for g in range(4)]
        bAB = [Buf(f"AB{k}") for k in range(8)]
        bPG = [Buf(f"PG{k}") for k in range(27)]
        bZ, bP, bQ = bPG[0:9], bPG[9:18], bPG[18:27]
        bFT = [Buf(f"FT{r}") for r in range(3)]
        bTT = [Buf(f"TT{r}") for r in range(2)]
        bGC = [Buf(f"GC{r}") for r in range(2)]
        bGBR = [Buf(f"GBR{r}") for r in range(2)]
        bCV = [Buf(f"CV{r}") for r in range(2)]
        bCB = [[Buf(f"CB{s}_{c}") for c in range(8)] for s in range(2)]
        bCAR = [Buf(f"CAR{c}") for c in range(8)]
        bSLOT = [Buf(f"SLOT{i}") for i in range(RING)]
        bCST = Buf("CST"); bWSPB = Buf("WSPB"); bICE = [Buf("ICE0"), Buf("ICE1")]
        bONES = Buf("ONES"); bE0 = Buf("E0")
        bRS1 = Buf("RS1"); bR1 = Buf("R1"); bR2 = Buf("R2"); bR2SQ = Buf("R2SQ"); bSQT = Buf("SQT"); bRSC = Buf("RSC")
        bVS = [Buf("VS0"), Buf("VS1")]
        bEDG = [Buf(f"EDG{i}") for i in range(8)]
        bBK = [Buf(f"BK{i}") for i in range(8)]
        bWBF = [Buf(f"WBF{g}") for g in range(8)]
        bU = bHSB[0:4]
        bVT = bHSB[4:6]
        bVN = [Buf(f"VN{r}") for r in range(4)]

        CVB = [2, 5, 8, 12, 18, 24, 32, NSLAB]

        def cvgroup(j):
            for gi, bnd in enumerate(CVB):
                if j < bnd:
                    return gi

        G = lambda idx, c: CST[:, C_G + idx * 8 + c: C_G + idx * 8 + c + 1]
        CWc = lambda k, c: CST[:, C_CW + k * 8 + c: C_CW + k * 8 + c + 1]
        BS3 = CST[:, C_BS:C_BS + 512].rearrange("p (h q) -> p h q", h=4)
        GVR = CST[:, C_GVR:C_GVR + 512]

        bank_rr = [0]

        def next_bank():
            b = 2 + bank_rr[0] % 6
            bank_rr[0] += 1
            return b

        rot = {"sq": 0, "sqn": 0, "ft": 0, "l1": 0, "v": 0}

        def rnext(k, n):
            r = rot[k] % n
            rot[k] += 1
            return r

        for j in range(NSLAB):
            g = cvgroup(j)
            u = SLAB_USED[j]
            P.dma("pool", (lambda e, j=j, u=u: e.dma_start(out=wbf[j, :, 0:u], in_=wsrc[j, :, 0:u])),
                  f"cv{g}", writes=[bWBF[g]], nodeps=True)
        P.dma("sp", lambda e: e.dma_start(out=CST[:, :], in_=cst_d[:, :]), "cst", writes=[bCST])
        P.dma("sp", lambda e: e.dma_start(out=HSB[:, 0:2, :].rearrange("p a b -> p (a b)"), in_=wsp_d[:, :]), "wsp",
              writes=[bHSB[0], bHSB[1]])
        P.op("dve", lambda e: e.tensor_copy(out=WSPB[:, :], in_=HSB[:, 0:2, :].rearrange("p a b -> p (a b)")),
             reads=[bHSB[0], bHSB[1]], writes=[bWSPB])
        P.op("pool", lambda e: e.memset(ONES[:, :], 1.0), writes=[bONES])
        P.op("pool", lambda e: e.memset(E0[:, :], 1.0 / 128.0), writes=[bE0])
        P.op("pool", lambda e: e.memset(CAR[:, :, :], 0.0), writes=bCAR)

        seq = []
        for it in range(NT + 1):
            if it < NT:
                seq += list(range(0, P1_SLABS))
            if it >= 1:
                seq += list(range(P1_SLABS, NSLAB))
        ring_state = {"loaded": 0, "cur": 0}

        def ring_load(k):
            if k >= len(seq):
                return
            j = seq[k]
            slot = k % RING
            u = SLAB_USED[j]
            reads = [bWBF[cvgroup(j)]]
            P.dma("sp", (lambda e, j=j, slot=slot, u=u: e.dma_start(out=RINGT[slot][:, 0:u], in_=wbf[j, :, 0:u])),
                  f"rg{slot}", reads=reads, writes=[bSLOT[slot]])

        def ring_acquire(expect):
            k = ring_state["cur"]
            assert seq[k] == expect, (k, seq[k], expect)
            return RINGT[k % RING], bSLOT[k % RING]

        def ring_release():
            k = ring_state["cur"]
            ring_state["cur"] += 1
            ring_load(k + RING)


        def mm(out, lhsT, rhs, start, stop):
            f = lambda e: e.matmul(out, lhsT, rhs, start=start, stop=stop)
            f.ncols = out.shape[-1]
            return f

        def stat_mm(bank, width, sqr, first, last, halves=None):
            if halves is None:
                fns = [mm(PS[:, bank, 0:width], ONES[:, :], SQ[sqr][:, 0:width], first, last)]
                P.group(fns, reads=[bSQ[sqr], bONES], writes=[bBK[bank]])
            else:
                hw = halves
                for h in range(2):
                    fns = [mm(PS[:, bank + h, 0:hw], ONES[:, :], SQ[sqr][:, h * hw:(h + 1) * hw], first, last)]
                    P.group(fns, reads=[bSQ[sqr], bONES], writes=[bBK[bank + h]])

        def rstd_from_bank(bank, width, dst, bdst):
            P.op("act", lambda e: e.activation(out=SQT[:, 0:width], in_=PS[:, bank, 0:width], func=AF.Sqrt,
                                               scale=1.0 / D, bias=EPS),
                 reads=[bBK[bank]], writes=[bSQT])
            P.op("dve", lambda e: e.reciprocal(out=dst[:, 0:width], in_=SQT[:, 0:width]), reads=[bSQT], writes=[bdst])

        def load_x(t):
            TW = tile_T[t] + 16
            P.dma("sp", lambda e: e.dma_start(out=XR[:, :, 0:TW], in_=xin[t].rearrange("p (c w) -> p c w", c=8)),
                  "xr", writes=bXR)
            i = t % 2
            P.dma("sp", lambda e: e.dma_start(out=ICE[i][:, :], in_=ice_d[:, t * 64:(t + 1) * 64]), f"ice{i}",
                  writes=[bICE[i]])

        def n1_stat_chunk(t, c):
            TW = tile_T[t] + 16
            r = rnext("sqn", 3)
            P.op("act", lambda e, c=c, r=r: e.activation(out=SQN[r][:, 0:TW], in_=XR[:, c, 0:TW], func=AF.Square),
                 reads=[bXR[c]], writes=[bSQN[r]])
            return r

        def n1_stat_mm(t, c, r):
            TW = tile_T[t] + 16; HW = TW // 2
            for h in range(2):
                fns = [mm(PS[:, h, 0:HW], ONES[:, :], SQN[r][:, h * HW:(h + 1) * HW], c == 0, c == 7)]
                P.group(fns, reads=[bSQN[r], bONES], writes=[bBK[h]])

        def n1_rstd(t):
            T = tile_T[t]; TW = T + 16; HW = TW // 2
            P.op("act", lambda e: e.activation(out=SQT[:, 0:TW].rearrange("p (h w) -> p h w", h=2),
                                               in_=PS[:, 0:2, 0:HW], func=AF.Sqrt, scale=1.0 / D, bias=EPS),
                 reads=[bBK[0], bBK[1]], writes=[bSQT])
            P.op("dve", lambda e: e.reciprocal(out=RS1[:, 0:TW], in_=SQT[:, 0:TW]), reads=[bSQT], writes=[bRS1])

        def n1_vcols(t):
            T = tile_T[t]
            b = next_bank()
            nj = T // 128
            for j in range(nj):
                P.group([mm(PS[:, b, j:j + 1], RS1[:, 8 + 128 * j: 8 + 128 * (j + 1)], E0[:, 0:1], True, True)],
                        reads=[bRS1, bE0], writes=[bBK[b]])
            P.op("dve", lambda e: e.tensor_copy(out=RSC[:, 0:nj], in_=PS[:, b, 0:nj]), reads=[bBK[b]], writes=[bRSC])

        def n1_xg(t):
            TW = tile_T[t] + 16
            for c in range(8):
                P.op("dve", lambda e, c=c: e.scalar_tensor_tensor(out=XG[:, c, 0:TW], in0=XR[:, c, 0:TW],
                                                                   scalar=G(0, c), in1=RS1[:, 0:TW],
                                                                   op0=ALU.mult, op1=ALU.mult),
                     reads=[bXR[c], bCST, bRS1], writes=[bXG[c]])

        def emit_n1(t):
            rs = [n1_stat_chunk(t, c) for c in range(3)]
            for c in range(8):
                n1_stat_mm(t, c, rs[c])
                if c + 3 < 8:
                    rs.append(n1_stat_chunk(t, c + 3))
            n1_rstd(t)
            n1_xg(t)

        def hobj(i):
            return [bHSB[i]] + ([bVN[i - 6], bVN[i - 4]] if i >= 6 else [])

        def out_stage(T, slabs_fn, nk, rhs_fn, rhs_bufs, gpost, gpre, res_in, b_res_in, xs, bxs, final=False):
            pending = None
            for i in range(8):
                lhs_fn, slotb, rel = slabs_fn(i)
                b = next_bank()
                fns = [mm(PS[:, b, 0:T], lhs_fn(kc), rhs_fn(kc), kc == 0, kc == nk - 1) for kc in range(nk)]
                P.group(fns, reads=[slotb], per=[[rhs_bufs[kc]] for kc in range(nk)], writes=[bBK[b]])
                if rel:
                    ring_release()
                if pending is not None:
                    stat_mm(0, T, pending[0], pending[1] == 0, False)
                r = rnext("sq", 3)
                P.op("act", lambda e, r=r, b=b: e.activation(out=SQ[r][:, 0:T], in_=PS[:, b, 0:T], func=AF.Square),
                     reads=[bBK[b]], writes=[bSQ[r]])
                P.op("act", lambda e, i=i, b=b: e.activation(out=HSB[:, i, 0:T], in_=PS[:, b, 0:T], func=AF.Identity,
                                                             scale=G(gpost, i)),
                     reads=[bBK[b], bCST], writes=hobj(i))
                pending = (r, i)
            stat_mm(0, T, pending[0], False, True)
            rstd_from_bank(0, T, R1, bR1)
            for c in range(8):
                P.op("dve", lambda e, c=c: e.tensor_tensor(out=HSB[:, c, 0:T], in0=HSB[:, c, 0:T], in1=R1[:, 0:T],
                                                           op=ALU.mult),
                     reads=hobj(c) + [bR1], writes=hobj(c))
                P.op("dve", lambda e, c=c: e.tensor_tensor(out=xs(c), in0=res_in(c), in1=HSB[:, c, 0:T], op=ALU.add),
                     reads=[b_res_in[c]] + hobj(c), writes=[bxs[c]])
                if not final:
                    P.op("act", lambda e, c=c: e.activation(out=XG[:, c, 8:8 + T], in_=xs(c), func=AF.Identity,
                                                            scale=G(gpre, c)),
                         reads=[bxs[c], bCST], writes=[bXG[c]])
                    r = rnext("sq", 3)
                    P.op("act", lambda e, c=c, r=r: e.activation(out=SQ[r][:, 0:T], in_=xs(c), func=AF.Square),
                         reads=[bxs[c]], writes=[bSQ[r]])
                    stat_mm(1, T, r, c == 0, c == 7)
            if not final:
                rstd_from_bank(1, T, R2, bR2)

        def ffn_stage(T, base, n1_next=None):
            n1r = {}
            pend = [None]

            def finish(p):
                f, bu, r = p
                P.op("dve", lambda e, r=r, bu=bu, f=f: e.tensor_tensor(out=Hc[f][:, 0:T], in0=PS[:, bu, 0:T],
                                                                        in1=FT[r][:, 0:T], op=ALU.mult),
                     reads=[bBK[bu], bFT[r]], writes=[bPG[f]])

            for j in range(11):
                slab, slotb = ring_acquire(base + j)
                s3 = slab[:, :].rearrange("p (k c) -> p k c", k=8)
                if n1_next is not None and j < 8:
                    n1r[j] = n1_stat_chunk(n1_next, j)
                for q in range(2):
                    f = 2 * j + q
                    bg = next_bank(); bu = next_bank()
                    fg = [mm(PS[:, bg, 0:T], s3[:, kc, q * 128:(q + 1) * 128], XG[:, kc, 8:8 + T], kc == 0, kc == 7)
                          for kc in range(8)]
                    P.group(fg, reads=[slotb], per=[[bXG[kc]] for kc in range(8)], writes=[bBK[bg]])
                    fu = [mm(PS[:, bu, 0:T], s3[:, kc, 256 + q * 128:256 + (q + 1) * 128], XG[:, kc, 8:8 + T],
                             kc == 0, kc == 7) for kc in range(8)]
                    P.group(fu, reads=[slotb], per=[[bXG[kc]] for kc in range(8)], writes=[bBK[bu]])
                    r = rnext("ft", 3)
                    P.op("dve", lambda e, r=r, bg=bg: e.tensor_tensor(out=FT[r][:, 0:T], in0=PS[:, bg, 0:T],
                                                                       in1=R2[:, 0:T], op=ALU.mult),
                         reads=[bBK[bg], bR2], writes=[bFT[r]])
                    P.op("act", lambda e, r=r: e.activation(out=FT[r][:, 0:T], in_=FT[r][:, 0:T], func=AF.Silu),
                         reads=[bFT[r]], writes=[bFT[r]])
                    P.op("pool", lambda e, r=r: e.tensor_tensor(out=FT[r][:, 0:T], in0=FT[r][:, 0:T], in1=R2[:, 0:T],
                                                                op=ALU.mult),
                         reads=[bFT[r], bR2], writes=[bFT[r]])
                    if pend[0] is not None:
                        finish(pend[0])
                    pend[0] = (f, bu, r)
                ring_release()
                if n1_next is not None and 1 <= j <= 8:
                    n1_stat_mm(n1_next, j - 1, n1r[j - 1])
            finish(pend[0])
            if n1_next is not None:
                n1_rstd(n1_next)
                n1_xg(n1_next)

        def down_slabs(base):
            def fn(i):
                slab, slotb = ring_acquire(base + i)
                return (lambda kc: slab[:, kc * 128:(kc + 1) * 128]), slotb, True
            return fn

        def o_slabs(base):
            st = {}

            def fn(i):
                if i % 4 == 0:
                    st["s"] = ring_acquire(base + i // 4)
                slab, slotb = st["s"]
                s3 = slab[:, :].rearrange("p (k c) -> p k c", k=8)
                col = (i % 4) * 128
                return (lambda kc: s3[:, kc, col:col + 128]), slotb, (i % 4 == 3)
            return fn

        def phase1(t):
            T = tile_T[t]; TW = T + 16; HW = TW // 2; s = t % 2
            nj = T // 128
            ic = ICE[t % 2]; bic = bICE[t % 2]
            slab, slotb = ring_acquire(0)
            s3 = slab[:, :].rearrange("p (k c) -> p k c", k=8)
            for g in range(4):
                b0 = (2 + 2 * g) % 8
                for h in range(2):
                    fns = [mm(PS[:, b0 + h, 0:HW], s3[:, kc, g * 128:(g + 1) * 128], XG[:, kc, h * HW:(h + 1) * HW],
                              kc == 0, kc == 7) for kc in range(8)]
                    P.group(fns, reads=[slotb], per=[[bXG[kc]] for kc in range(8)], writes=[bBK[b0 + h]])
                P.op("act", lambda e, g=g, b0=b0: e.activation(
                    out=Zt[:, g, 0:TW].rearrange("p (h w) -> p h w", h=2), in_=PS[:, b0:b0 + 2, 0:HW], func=AF.Copy),
                     reads=[bBK[b0], bBK[b0 + 1]], writes=bZ)
            ring_release()
            bank_rr[0] = 0
            P.op("pool", lambda e: e.tensor_tensor(out=Pt[:, :, 1:TW], in0=Zt[:, :, 1:TW], in1=Zt[:, :, 0:TW - 1],
                                                   op=ALU.add), reads=bZ, writes=bP)
            P.op("pool", lambda e: e.tensor_tensor(out=Qt[:, 1:4, 2:TW - 1], in0=Pt[:, 1:4, 1:TW - 2],
                                                   in1=Pt[:, 1:4, 3:TW], op=ALU.add), reads=bP, writes=bQ)
            P.op("pool", lambda e: e.tensor_tensor(out=Pt[:, 2:4, 4:TW - 3], in0=Qt[:, 2:4, 2:TW - 5],
                                                   in1=Qt[:, 2:4, 6:TW - 1], op=ALU.add), reads=bQ, writes=bP)
            P.op("pool", lambda e: e.tensor_tensor(out=Qt[:, 3, 8:TW - 7], in0=Pt[:, 3, 4:TW - 11],
                                                   in1=Pt[:, 3, 12:TW - 3], op=ALU.add), reads=bP, writes=bQ)
            for g in range(4):
                S = Pt if g % 2 == 0 else Qt
                bS = bP if g % 2 == 0 else bQ
                w = 2 ** (g + 1)
                P.op("dve", lambda e, g=g, S=S, w=w: e.scalar_tensor_tensor(
                    out=Rt[:, g, 0:T], in0=S[:, g, 8:8 + T], scalar=1.0 / w, in1=Zt[:, g, 8:8 + T],
                    op0=ALU.mult, op1=ALU.subtract), reads=bS + bZ, writes=[bR[g]])
            for g in range(4):
                S = Pt if g % 2 == 0 else Qt
                bS = bP if g % 2 == 0 else bQ
                for ed in range(2):
                    c0 = 0 if ed == 0 else T - 8
                    k = g * 2 + ed
                    P.op("pool", lambda e, g=g, S=S, c0=c0, ed=ed, k=k: e.tensor_tensor(
                        out=EDG[:, k * 8:k * 8 + 8], in0=S[:, g, 8 + c0:16 + c0],
                        in1=ic[:, g * 16 + ed * 8: g * 16 + ed * 8 + 8], op=ALU.mult),
                         reads=bS + [bic], writes=[bEDG[k]])
            for g in range(4):
                for ed in range(2):
                    c0 = 0 if ed == 0 else T - 8
                    k = g * 2 + ed
                    P.op("pool", lambda e, g=g, c0=c0, k=k: e.tensor_tensor(
                        out=Rt[:, g, c0:c0 + 8], in0=EDG[:, k * 8:k * 8 + 8], in1=Zt[:, g, 8 + c0:16 + c0],
                        op=ALU.subtract), reads=[bEDG[k]] + bZ, writes=[bR[g]])
            slab, slotb = ring_acquire(1)
            s3 = slab[:, :].rearrange("p (k c) -> p k c", k=8)
            for jc in range(nj):
                b = next_bank()
                r = jc % 2
                fns = [mm(PS[:, b, 0:512], XG[:, kc, 8 + 128 * jc: 8 + 128 * (jc + 1)], s3[:, kc, 0:512],
                          kc == 0, kc == 7) for kc in range(8)]
                P.group(fns, reads=[slotb], per=[[bXG[kc]] for kc in range(8)], writes=[bBK[b]])
                VT = HSB[:, 4 + r, :]
                P.op("act", lambda e, b=b, jc=jc, VT=VT: e.activation(out=VT, in_=PS[:, b, 0:512],
                                                                       func=AF.Gelu_apprx_tanh),
                     reads=[bBK[b]], writes=[bVT[r]])
                for h in range(4):
                    P.op("dve", lambda e, h=h, r=r, VT=VT: e.bn_stats(out=ST[r][:, h, :], in_=VT[:, h * 128:(h + 1) * 128]),
                         reads=[bVT[r]], writes=[bVS[r]])
                    P.op("dve", lambda e, h=h, r=r: e.bn_aggr(out=MV[r][:, h, :], in_=ST[r][:, h, :]),
                         reads=[bVS[r]], writes=[bVS[r]])
                P.op("act", lambda e, r=r: e.activation(out=SV[r][:, :], in_=MV[r][:, :, 1], func=AF.Sqrt, bias=EPS),
                     reads=[bVS[r]], writes=[bVS[r]])
                P.op("dve", lambda e, r=r: e.reciprocal(out=RV[r][:, :], in_=SV[r][:, :]), reads=[bVS[r]],
                     writes=[bVS[r]])
                for h in range(4):
                    P.op("dve", lambda e, h=h, r=r, VT=VT: e.tensor_scalar(
                        out=VT[:, h * 128:(h + 1) * 128], in0=VT[:, h * 128:(h + 1) * 128],
                        scalar1=MV[r][:, h, 0:1], scalar2=RV[r][:, h:h + 1], op0=ALU.subtract, op1=ALU.mult),
                         reads=[bVT[r], bVS[r]], writes=[bVT[r]])
                P.op("pool", lambda e, jc=jc, VT=VT: e.tensor_tensor(out=VN[jc], in0=VT, in1=GVR, op=ALU.mult),
                     reads=[bVT[r], bCST], writes=[bVN[jc]])
            ring_release()
            slab, slotb = ring_acquire(2)
            s3 = slab[:, :].rearrange("p (k c) -> p k c", k=8)
            for j in range(4):
                b = next_bank()
                fns = [mm(PS[:, b, 0:T], s3[:, kc, j * 128:(j + 1) * 128], XG[:, kc, 8:8 + T], kc == 0, kc == 7)
                       for kc in range(8)]
                P.group(fns, reads=[slotb], per=[[bXG[kc]] for kc in range(8)], writes=[bBK[b]])
                P.op("act", lambda e, j=j, b=b: e.activation(out=HSB[:, j, 0:T], in_=PS[:, b, 0:T],
                                                             func=AF.Gelu_apprx_tanh), reads=[bBK[b]], writes=[bU[j]])
            ring_release()
            for g in range(4):
                b = next_bank()
                P.group([mm(PS[:, b, 0:T], WP[:, g, :], Rt[:, g, 0:T], True, True)], reads=[bWSPB, bR[g]],
                        writes=[bBK[b]])
                P.op("act", lambda e, g=g, b=b: e.activation(out=AB[:, 4 + g, 0:T], in_=PS[:, b, 0:T], func=AF.Identity,
                                                             scale=CST[:, C_PSC + g:C_PSC + g + 1]),
                     reads=[bBK[b], bCST], writes=[bAB[4 + g]])
            for jc in range(nj):
                b2 = next_bank()
                for h in range(4):
                    P.group([mm(PS[:, b2, h * 128:(h + 1) * 128], VN[jc][:, h * 128:(h + 1) * 128], WS[:, h, :],
                                True, True)], reads=[bVN[jc], bWSPB], writes=[bBK[b2]])
                rf = rnext("ft", 3)
                P.op("dve", lambda e, b2=b2, rf=rf: e.tensor_tensor(
                    out=FT[rf][:, :].rearrange("p (h q) -> p h q", h=4),
                    in0=PS[:, b2, :].rearrange("p (h q) -> p h q", h=4), in1=BS3, op=ALU.add),
                     reads=[bBK[b2], bCST], writes=[bFT[rf]])
                P.op("dve", lambda e, rf=rf, jc=jc: e.tensor_tensor(
                    out=AB[:, 0:4, jc * 128:(jc + 1) * 128], in0=FT[rf][:, :].rearrange("p (h q) -> p h q", h=4),
                    in1=HSB[:, 0:4, jc * 128:(jc + 1) * 128], op=ALU.mult),
                     reads=[bFT[rf]] + bU, writes=bAB[0:4])
            out_stage(T, o_slabs(3), 8, lambda kc: AB[:, kc, 0:T], bAB, gpost=1, gpre=2,
                      res_in=lambda c: XR[:, c, 8:8 + T], b_res_in=bXR, xs=lambda c: X[s][:, c, 0:T], bxs=bX[s])
            if t + 1 < NT:
                load_x(t + 1)
            ffn_stage(T, 5)
            out_stage(T, down_slabs(16), NFF, lambda kc: Hc[kc][:, 0:T], bPG[0:NFF], gpost=3, gpre=4,
                      res_in=lambda c: X[s][:, c, 0:T], b_res_in=bX[s], xs=lambda c: X[s][:, c, 0:T], bxs=bX[s])
            P.op("pool", lambda e: e.tensor_tensor(out=R2SQ[:, 0:T], in0=R2[:, 0:T], in1=R2[:, 0:T], op=ALU.mult),
                 reads=[bR2], writes=[bR2SQ])
            has_left = t not in SEG_START_L
            has_right = t not in SEG_END_L
            for c in range(8):
                slab, slotb = ring_acquire(24 + c)
                s3 = slab[:, 0:NKC * 384].rearrange("p (k c) -> p k c", k=8)
                banks = [next_bank() for _ in range(3)]
                for w in range(3):
                    fns = [mm(PS[:, banks[w], 0:T], s3[:, kc, w * 128:(w + 1) * 128], XG[:, kc, 8:8 + T],
                              kc == 0, kc == 7) for kc in range(8)]
                    P.group(fns, reads=[slotb], per=[[bXG[kc]] for kc in range(8)], writes=[bBK[banks[w]]])
                ring_release()
                r = rnext("l1", 2)
                bB, bC, bV = banks
                P.op("dve", lambda e, r=r, bC=bC: e.tensor_tensor(out=GC[r][:, 0:T], in0=PS[:, bC, 0:T], in1=R2SQ[:, 0:T],
                                                                   op=ALU.mult),
                     reads=[bBK[bC], bR2SQ], writes=[bGC[r]])
                P.op("dve", lambda e, r=r, bB=bB: e.tensor_tensor(out=GBR[r][:, 0:T], in0=PS[:, bB, 0:T],
                                                                   in1=R2[:, 0:T], op=ALU.mult),
                     reads=[bBK[bB], bR2], writes=[bGBR[r]])
                P.op("dve", lambda e, r=r, bV=bV: e.tensor_tensor(out=TTt[r][:, 1:T + 1], in0=PS[:, bV, 0:T],
                                                                   in1=GC[r][:, 0:T], op=ALU.mult),
                     reads=[bBK[bV], bGC[r]], writes=[bTT[r]])
                if has_left:
                    sp_ = (t - 1) % 2
                    Tp = tile_T[t - 1]
                    P.op("pool", lambda e, c=c, r=r: e.tensor_scalar(out=CAR[:, c, 3:4], in0=TTt[r][:, 1:2],
                                                                      scalar1=CWc(2, c), scalar2=None, op0=ALU.mult),
                         reads=[bTT[r], bCST], writes=[bCAR[c]])
                    P.op("pool", lambda e, c=c: e.tensor_tensor(out=CAR[:, c, 3:4], in0=CAR[:, c, 3:4],
                                                                in1=CAR[:, c, 1:2], op=ALU.add),
                         reads=[bCAR[c]], writes=[bCAR[c]])
                    P.op("pool", lambda e, c=c, sp_=sp_, Tp=Tp: e.tensor_tensor(
                        out=CB[sp_][:, c, Tp - 1:Tp], in0=CAR[:, c, 3:4], in1=CAR[:, c, 2:3], op=ALU.mult),
                         reads=[bCAR[c]], writes=[bCB[sp_][c]])
                    P.op("pool", lambda e, c=c, r=r: e.tensor_copy(out=TTt[r][:, 0:1], in_=CAR[:, c, 0:1]),
                         reads=[bCAR[c]], writes=[bTT[r]])
                else:
                    P.op("pool", lambda e, r=r: e.memset(TTt[r][:, 0:1], 0.0), writes=[bTT[r]])
                P.op("dve", lambda e, c=c, r=r: e.tensor_scalar(out=CV[r][:, 0:T - 1], in0=TTt[r][:, 0:T - 1],
                                                                 scalar1=CWc(0, c), scalar2=None, op0=ALU.mult),
                     reads=[bTT[r], bCST], writes=[bCV[r]])
                P.op("dve", lambda e, c=c, r=r: e.scalar_tensor_tensor(out=CV[r][:, 0:T - 1], in0=TTt[r][:, 1:T],
                                                                        scalar=CWc(1, c), in1=CV[r][:, 0:T - 1],
                                                                        op0=ALU.mult, op1=ALU.add),
                     reads=[bTT[r], bCV[r], bCST], writes=[bCV[r]])
                P.op("dve", lambda e, c=c, r=r: e.scalar_tensor_tensor(out=CV[r][:, 0:T - 1], in0=TTt[r][:, 2:T + 1],
                                                                        scalar=CWc(2, c), in1=CV[r][:, 0:T - 1],
                                                                        op0=ALU.mult, op1=ALU.add),
                     reads=[bTT[r], bCV[r], bCST], writes=[bCV[r]])
                P.op("pool", lambda e, c=c, r=r: e.tensor_tensor(out=CB[s][:, c, 0:T - 1], in0=GBR[r][:, 0:T - 1],
                                                                 in1=CV[r][:, 0:T - 1], op=ALU.mult),
                     reads=[bGBR[r], bCV[r]], writes=[bCB[s][c]])
                P.op("act", lambda e, c=c, r=r: e.activation(out=CAR[:, c, 0:1], in_=TTt[r][:, T:T + 1], func=AF.Copy),
                     reads=[bTT[r]], writes=[bCAR[c]])
                P.op("pool", lambda e, c=c, r=r: e.tensor_scalar(out=CAR[:, c, 1:2], in0=TTt[r][:, T - 1:T],
                                                                  scalar1=CWc(0, c), scalar2=None, op0=ALU.mult),
                     reads=[bTT[r], bCST], writes=[bCAR[c]])
                P.op("pool", lambda e, c=c, r=r: e.tensor_scalar(out=CAR[:, c, 3:4], in0=TTt[r][:, T:T + 1],
                                                                  scalar1=CWc(1, c), scalar2=None, op0=ALU.mult),
                     reads=[bTT[r], bCST], writes=[bCAR[c]])
                P.op("pool", lambda e, c=c: e.tensor_tensor(out=CAR[:, c, 1:2], in0=CAR[:, c, 1:2],
                                                            in1=CAR[:, c, 3:4], op=ALU.add),
                     reads=[bCAR[c]], writes=[bCAR[c]])
                P.op("act", lambda e, c=c, r=r: e.activation(out=CAR[:, c, 2:3], in_=GBR[r][:, T - 1:T], func=AF.Copy),
                     reads=[bGBR[r]], writes=[bCAR[c]])
                if not has_right:
                    P.op("pool", lambda e, c=c: e.tensor_tensor(out=CB[s][:, c, T - 1:T], in0=CAR[:, c, 1:2],
                                                                in1=CAR[:, c, 2:3], op=ALU.mult),
                         reads=[bCAR[c]], writes=[bCB[s][c]])

        def phase2(t, n1_next):
            T = tile_T[t]; s = t % 2
            out_stage(T, o_slabs(32), 8, lambda kc: CB[s][:, kc, 0:T], bCB[s], gpost=5, gpre=6,
                      res_in=lambda c: X[s][:, c, 0:T], b_res_in=bX[s], xs=lambda c: X[s][:, c, 0:T], bxs=bX[s])
            ffn_stage(T, 34, n1_next)
            out_stage(T, down_slabs(45), NFF, lambda kc: Hc[kc][:, 0:T], bPG[0:NFF], gpost=7, gpre=None,
                      res_in=lambda c: X[s][:, c, 0:T], b_res_in=bX[s], xs=lambda c: X[s][:, c, 0:T], bxs=bX[s],
                      final=True)
            P.dma("sp", lambda e: e.dma_start(out=yout[t].rearrange("p (c w) -> p c w", c=8), in_=X[s][:, :, 0:T]),
                  f"y{s}", reads=bX[s])

        SEG_START_L = [i for i in SEG_START if i < NT]
        SEG_END_L = [i for i in SEG_END if i < NT] + [NT - 1]

        for k in range(RING):
            ring_load(k)
        load_x(0)
        emit_n1(0)
        for it in range(NT + 1):
            if it < NT:
                phase1(it)
                if it == 0 and NT > 1:
                    emit_n1(1)
            if it >= 1:
                nxt = it + 1 if (it + 1 < NT) else None
                phase2(it - 1, nxt)
        for s in range(2):
            if P.dcnt.get(f"y{s}", 0):
                P.final_wait("sp", f"y{s}", P.dcnt[f"y{s}"])

        engmap = {"pe": "tensor", "act": "scalar", "dve": "vector", "pool": "gpsimd", "sp": "sync"}
        with nc.Block() as block:
            def make(engname):
                items = P.lists[engname]

                def body(e):
                    for it_ in items:
                        if it_[0] == "wait":
                            e.wait_ge(sems[it_[1]], it_[2])
                        else:
                            ins = it_[1](e)
                            if it_[2] is not None:
                                ins.then_inc(sems[it_[2]], it_[3])
                return body
            for en, attr in engmap.items():
                getattr(block, attr)(make(en))
        build_nc.stats = {k: len(v) for k, v in P.lists.items()}
        build_nc.P = P
    return nc


_NC_CACHE = {}


def prepare_core_inputs(core, xall, tile_T, shared):
    tiles, _ = core_plan(core)
    m = dict(shared)
    NT = len(tile_T)
    ice = np.zeros((128, NT * 64), np.float32)
    for t in range(NT):
        sq, off, T = tiles[t]
        TW = T + 16
        lo, hi = off - 8, off + T + 8
        buf = np.zeros((TW, D), np.float32)
        a, b = max(lo, 0), min(hi, SEQ)
        buf[a - lo:b - lo] = xall[sq, a:b]
        m[f"xin{t}"] = np.ascontiguousarray(buf.reshape(TW, 8, 128).transpose(2, 1, 0)).reshape(128, 8 * TW)
        for g in range(4):
            h = 2 ** g
            for ed in range(2):
                pos = off + (np.arange(8) if ed == 0 else T - 8 + np.arange(8))
                cnt = np.minimum(pos + h, SEQ) - np.maximum(pos - h, 0)
                ice[:, t * 64 + g * 16 + ed * 8: t * 64 + g * 16 + ed * 8 + 8] = (1.0 / cnt).astype(np.float32)
    m["ice"] = ice
    return m


def run_cores(inputs, tile_T, n_cores=N_CORES, trace=False):
    key = tuple(tile_T)
    if key not in _NC_CACHE:
        _NC_CACHE[key] = build_nc(list(tile_T))
    nc = _NC_CACHE[key]
    xall = np.concatenate([np.asarray(inputs["x_prompt"], np.float32), np.asarray(inputs["x_sample"], np.float32)], 0)
    inp = {k: np.asarray(v, np.float32) for k, v in inputs.items() if not k.startswith("x_")}
    shared = {"wsrc": build_wsrc(inp), "cst": build_cst(inp), "wsp": build_wsp(inp)}
    in_maps = [prepare_core_inputs(c, xall, tile_T, shared) for c in range(n_cores)]
    res = run_bass_kernel_spmd(nc, in_maps, core_ids=list(range(n_cores)), trace=trace)
    return res, xall


def kernel(**inputs):
    res, xall = run_cores(inputs, TILE_T)
    yall = np.zeros((12, SEQ, D), np.float32)
    for core in range(N_CORES):
        tiles, base = core_plan(core)
        r = res.results[core]
        for t, (sq, off, T) in enumerate(tiles):
            if t >= 8:
                if core % 2 == 0 and off >= 2048:
                    continue
                if core % 2 == 1 and off < 2048:
                    continue
            y = np.asarray(r[f"y{t}"]).reshape(128, 8, T)
            yall[sq, off:off + T] = y.transpose(2, 1, 0).reshape(T, D)
    return yall[:8].copy(), yall[8:].copy()
```

```python
import contextlib
import sys as _sys
import numpy as np
import concourse.bass as bass
import concourse.mybir as mybir
from concourse.bass_utils import run_bass_kernel_spmd

F32 = mybir.dt.float32
BF16 = mybir.dt.bfloat16
AF = mybir.ActivationFunctionType
ALU = mybir.AluOpType

D = 1024
SEQ = 4096
DFF = 2816
NKC = 8
NFF = 22
EPS = 1e-6
N_CORES = 8

FULL_TILES = [512] * 8
PART_TILES = [128, 512, 512, 512, 384, 128]
TILE_T = FULL_TILES + PART_TILES
NT_ALL = len(TILE_T)
SEG_START = (0, 8)
SEG_END = (7, 13)

SLAB_COLS = 4096
P1_SLABS = 3 + 2 + 11 + 8 + 8
P2_SLABS = 2 + 11 + 8
NSLAB = P1_SLABS + P2_SLABS
RING = 4


def _slab_used():
    u = [4096] * 5 + [4096] * 11 + [NFF * 128] * 8 + [NKC * 384] * 8
    u += [4096] * 2 + [4096] * 11 + [NFF * 128] * 8
    return u


SLAB_USED = _slab_used()


def _pack_k(w, cols):
    k = w.shape[0]
    sub = w[:, cols].reshape(k // 128, 128, len(cols))
    return np.ascontiguousarray(sub.transpose(1, 0, 2)).reshape(128, -1)


def build_wsrc(inp):
    ws = np.zeros((NSLAB, 128, SLAB_COLS), np.float32)
    ar = np.arange
    s = 0
    w_in = inp["ab_w_in"][0]
    ws[s, :, :4096] = _pack_k(w_in, ar(1024, 1536)); s += 1
    ws[s, :, :4096] = _pack_k(w_in, ar(512, 1024)); s += 1
    ws[s, :, :4096] = _pack_k(w_in, ar(0, 512)); s += 1
    w_o = inp["ab_w_out"][0]
    ws[s, :, :4096] = _pack_k(w_o, ar(0, 512)); s += 1
    ws[s, :, :4096] = _pack_k(w_o, ar(512, 1024)); s += 1

    def ffn(layer, s):
        wg, wu, wd = inp["ffn_w_gate"][layer], inp["ffn_w_up"][layer], inp["ffn_w_down"][layer]
        for j in range(11):
            g = _pack_k(wg, ar(256 * j, 256 * j + 256)).reshape(128, 8, 256)
            u = _pack_k(wu, ar(256 * j, 256 * j + 256)).reshape(128, 8, 256)
            ws[s, :, :4096] = np.concatenate([g, u], axis=2).reshape(128, 4096); s += 1
        for i in range(8):
            ws[s, :, :NFF * 128] = _pack_k(wd, ar(128 * i, 128 * i + 128)); s += 1
        return s

    s = ffn(0, s)
    w_c = inp["c_w_in"][0]
    for c in range(8):
        cols = np.concatenate([ar(c * 128, c * 128 + 128), ar(1024 + c * 128, 1024 + c * 128 + 128),
                               ar(2048 + c * 128, 2048 + c * 128 + 128)])
        ws[s, :, :NKC * 384] = _pack_k(w_c, cols); s += 1
    w_co = inp["c_w_out"][0]
    ws[s, :, :4096] = _pack_k(w_co, ar(0, 512)); s += 1
    ws[s, :, :4096] = _pack_k(w_co, ar(512, 1024)); s += 1
    s = ffn(1, s)
    assert s == NSLAB
    return ws


C_G = 0
C_PSC = 64
C_CW = 68
C_BS = 92
C_GVR = 604
NCST = 1116


def build_cst(inp):
    c = np.zeros((128, NCST), np.float32)
    ng = inp["norm_g"].reshape(8, 8, 128)
    c[:, C_G:C_G + 64] = ng.transpose(2, 0, 1).reshape(128, 64)
    c[:, C_PSC:C_PSC + 4] = inp["ab_pool_scale"][0].reshape(4, 128).T
    c[:, C_CW:C_CW + 24] = inp["c_conv_w"][0].reshape(3, 8, 128).transpose(2, 0, 1).reshape(128, 24)
    c[:, C_BS:C_BS + 512] = inp["ab_b_spatial"][0].reshape(1, 512)
    c[:, C_GVR:C_GVR + 512] = inp["ab_v_norm_g"][0].reshape(1, 512)
    return c


def build_wsp(inp):
    w = np.zeros((128, 1024), np.float32)
    ws = inp["ab_w_spatial"][0]
    w[:, 0:512] = ws.transpose(2, 0, 1).reshape(128, 512)
    wp = inp["ab_w_pool"][0]
    w[:, 512:1024] = wp.transpose(1, 0, 2).reshape(128, 512)
    return w


def core_plan(core):
    p = core // 2
    if core % 2 == 0:
        full_seq, part_seq, base = 3 * p, 3 * p + 1, 0
    else:
        full_seq, part_seq, base = 3 * p + 2, 3 * p + 1, 1920
    tiles = []
    off = 0
    for T in FULL_TILES:
        tiles.append((full_seq, off, T)); off += T
    off = base
    for T in PART_TILES:
        tiles.append((part_seq, off, T)); off += T
    return tiles, base


class Buf:
    __slots__ = ("name", "w", "r")

    def __init__(self, name):
        self.name = name
        self.w = None
        self.r = []


class Prog:
    ENGS = ("pe", "act", "dve", "pool", "sp")

    def __init__(self):
        self.lists = {e: [] for e in self.ENGS}
        self.cnt = {e: 0 for e in self.ENGS}
        self.known = {e: {} for e in self.ENGS}
        self.dcnt = {}
        self.toklog = {}

    def _deps(self, eng, reads, writes):
        need = {}
        for b in reads:
            if b.w is not None:
                k, v = b.w
                need[k] = max(need.get(k, 0), v)
        for b in writes:
            if b.w is not None:
                k, v = b.w
                need[k] = max(need.get(k, 0), v)
            for (k, v) in b.r:
                need[k] = max(need.get(k, 0), v)
        kn = self.known[eng]
        for k, v in need.items():
            if kn.get(k, 0) < v:
                kn[k] = v
                self.lists[eng].append(("wait", k, v))

    def _mark(self, tok, reads, writes):
        for b in reads:
            b.r.append(tok)
        for b in writes:
            b.w = tok
            b.r = []

    def op(self, eng, fn, reads=(), writes=()):
        self._deps(eng, reads, writes)
        self.cnt[eng] += 1
        tok = (eng, self.cnt[eng])
        self.lists[eng].append(("ins", fn, eng, 1))
        self.toklog[tok] = _sys._getframe(1).f_lineno
        self._mark(tok, reads, writes)
        return tok

    def group(self, fns, reads=(), writes=(), per=None):
        self._deps("pe", reads, writes)
        self.cnt["pe"] += 1
        tok = ("pe", self.cnt["pe"])
        n = len(fns)
        allr = list(reads)
        for i, f in enumerate(fns):
            if per is not None:
                self._deps("pe", per[i], ())
                allr += list(per[i])
            last = i == n - 1
            self.lists["pe"].append(("ins", f, "pe" if last else None, 1 if last else 0))
        self.toklog[tok] = _sys._getframe(1).f_lineno
        self._mark(tok, allr, writes)
        return tok

    def dma(self, eng, fn, dsem, reads=(), writes=(), nodeps=False):
        if not nodeps:
            self._deps(eng, reads, writes)
        self.dcnt[dsem] = self.dcnt.get(dsem, 0) + 16
        tok = (dsem, self.dcnt[dsem])
        self.lists[eng].append(("ins", fn, dsem, 16))
        self.toklog[tok] = _sys._getframe(1).f_lineno
        self._mark(tok, reads, writes)
        return tok

    def final_wait(self, eng, key, val):
        self.lists[eng].append(("wait", key, val))


def build_nc(tile_T):
    NT = len(tile_T)
    nc = bass.Bass("TRN2", target_bir_lowering=False)
    xin = [nc.dram_tensor(f"xin{t}", [128, 8 * (tile_T[t] + 16)], F32, kind="ExternalInput").ap() for t in range(NT)]
    yout = [nc.dram_tensor(f"y{t}", [128, 8 * tile_T[t]], F32, kind="ExternalOutput").ap() for t in range(NT)]
    wsrc = nc.dram_tensor("wsrc", [NSLAB, 128, SLAB_COLS], F32, kind="ExternalInput").ap()
    cst_d = nc.dram_tensor("cst", [128, NCST], F32, kind="ExternalInput").ap()
    wsp_d = nc.dram_tensor("wsp", [128, 1024], F32, kind="ExternalInput").ap()
    ice_d = nc.dram_tensor("ice", [128, NT * 64], F32, kind="ExternalInput").ap()
    wbf = nc.dram_tensor("wbf", [NSLAB, 128, SLAB_COLS], BF16).ap()

    P = Prog()
    es = contextlib.ExitStack()

    def sb(name, shape, dt):
        return es.enter_context(nc.sbuf_tensor(name, shape, dt))

    with es:
        XR = sb("XR", [128, 8, 528], F32)
        X = [sb(f"X{s}", [128, 8, 512], F32) for s in range(2)]
        XG = sb("XG", [128, 8, 528], BF16)
        HSB = sb("HSB", [128, 8, 512], F32)
        SQ = [sb(f"SQ{r}", [128, 528], BF16) for r in range(3)]
        SQN = [sb(f"SQN{r}", [128, 528], BF16) for r in range(3)]
        Rt = sb("R", [128, 4, 512], BF16)
        AB = sb("AB", [128, 8, 512], BF16)
        ARENA = sb("ARENA", [128, 27 * 256], F32)
        FT = [sb(f"FT{r}", [128, 512], F32) for r in range(3)]
        TTt = [sb(f"TT{r}", [128, 528], F32) for r in range(2)]
        GC = [sb(f"GC{r}", [128, 512], F32) for r in range(2)]
        GBR = [sb(f"GBR{r}", [128, 512], F32) for r in range(2)]
        CV = [sb(f"CV{r}", [128, 512], F32) for r in range(2)]
        CB = [sb(f"CB{s}", [128, 8, 512], BF16) for s in range(2)]
        CAR = sb("CAR", [128, 8, 4], F32)
        RINGT = [sb(f"RG{i}", [128, SLAB_COLS], BF16) for i in range(RING)]
        CST = sb("CST", [128, NCST], F32)
        WSPB = sb("WSPB", [128, 1024], BF16)
        ICE = [sb(f"ICE{i}", [128, 64], F32) for i in range(2)]
        ONES = sb("ONES", [128, 128], BF16)
        E0 = sb("E0", [128, 2], F32)
        RS1 = sb("RS1", [128, 528], F32)
        R1 = sb("R1", [128, 512], F32)
        R2 = sb("R2", [128, 512], F32)
        R2SQ = sb("R2SQ", [128, 512], F32)
        SQT = sb("SQT", [128, 528], F32)
        RSC = sb("RSC", [128, 4], F32)
        ST = [sb(f"ST{r}", [128, 4, 6], F32) for r in range(2)]
        MV = [sb(f"MV{r}", [128, 4, 2], F32) for r in range(2)]
        SV = [sb(f"SV{r}", [128, 4], F32) for r in range(2)]
        RV = [sb(f"RV{r}", [128, 4], F32) for r in range(2)]
        EDG = sb("EDG", [128, 64], F32)
        PS = es.enter_context(nc.psum_tensor("PS", [128, 8, 512], F32))

        sem_names = ["pe", "act", "dve", "pool", "cv0", "cv1", "cv2", "cv3", "cv4", "cv5", "cv6", "cv7", "xr", "y0", "y1", "cst", "wsp", "ice0", "ice1"] + \
                    [f"rg{i}" for i in range(RING)]
        sems = {n: es.enter_context(nc.semaphore(n)) for n in sem_names}

        Hc = [ARENA[:, k * 256:(k + 1) * 256].bitcast(BF16) for k in range(NFF)]
        Zt = ARENA[:, 0:2112].rearrange("p (g w) -> p g w", g=4)
        Pt = ARENA[:, 2304:2304 + 2112].rearrange("p (g w) -> p g w", g=4)
        Qt = ARENA[:, 4608:4608 + 2112].rearrange("p (g w) -> p g w", g=4)
        WS = WSPB[:, 0:512].rearrange("p (h q) -> p h q", h=4)
        WP = WSPB[:, 512:1024].rearrange("p (g d) -> p g d", g=4)
        VN = [HSB[:, 6 + (r % 2), :].bitcast(BF16)[:, (r // 2) * 512:(r // 2) * 512 + 512] for r in range(4)]

        bXR = [Buf(f"XR{c}") for c in range(8)]
        bX = [[Buf(f"X{s}_{c}") for c in range(8)] for s in range(2)]
        bXG = [Buf(f"XG{c}") for c in range(8)]
        bHSB = [Buf(f"HSB{c}") for c in range(8)]
        bSQ = [Buf(f"SQ{r}") for r in range(3)]
        bSQN = [Buf(f"SQN{r}") for r in range(3)]
        bR = [Buf(f"R{g}") for g in range(4)]
        bAB = [Buf(f"AB{k}") for k in range(8)]
        bPG = [Buf(f"PG{k}") for k in range(27)]
        bZ, bP, bQ = bPG[0:9], bPG[9:18], bPG[18:27]
        bFT = [Buf(f"FT{r}") for r in range(3)]
        bTT = [Buf(f"TT{r}") for r in range(2)]
        bGC = [Buf(f"GC{r}") for r in range(2)]
        bGBR = [Buf(f"GBR{r}") for r in range(2)]
        bCV = [Buf(f"CV{r}") for r in range(2)]
        bCB = [[Buf(f"CB{s}_{c}") for c in range(8)] for s in range(2)]
        bCAR = [Buf(f"CAR{c}") for c in range(8)]
        bSLOT = [Buf(f"SLOT{i}") for i in range(RING)]
        bCST = Buf("CST"); bWSPB = Buf("WSPB"); bICE = [Buf("ICE0"), Buf("ICE1")]
        bONES = Buf("ONES"); bE0 = Buf("E0")
        bRS1 = Buf("RS1"); bR1 = Buf("R1"); bR2 = Buf("R2"); bR2SQ = Buf("R2SQ"); bSQT = Buf("SQT"); bRSC = Buf("RSC")
        bVS = [Buf("VS0"), Buf("VS1")]
        bEDG = [Buf(f"EDG{i}") for i in range(8)]
        bBK = [Buf(f"BK{i}") for i in range(8)]
        bWBF = [Buf(f"WBF{g}") for g in range(8)]
        bU = bHSB[0:4]
        bVT = bHSB[4:6]
        bVN = [Buf(f"VN{r}") for r in range(4)]

        CVB = [2, 5, 8, 12, 18, 24, 32, NSLAB]

        def cvgroup(j):
            for gi, bnd in enumerate(CVB):
                if j < bnd:
                    return gi

        G = lambda idx, c: CST[:, C_G + idx * 8 + c: C_G + idx * 8 + c + 1]
        CWc = lambda k, c: CST[:, C_CW + k * 8 + c: C_CW + k * 8 + c + 1]
        BS3 = CST[:, C_BS:C_BS + 512].rearrange("p (h q) -> p h q", h=4)
        GVR = CST[:, C_GVR:C_GVR + 512]

        bank_rr = [0]

        def next_bank():
            b = 2 + bank_rr[0] % 6
            bank_rr[0] += 1
            return b

        rot = {"sq": 0, "sqn": 0, "ft": 0, "l1": 0, "v": 0}

        def rnext(k, n):
            r = rot[k] % n
            rot[k] += 1
            return r

        for j in range(NSLAB):
            g = cvgroup(j)
            u = SLAB_USED[j]
            P.dma("pool", (lambda e, j=j, u=u: e.dma_start(out=wbf[j, :, 0:u], in_=wsrc[j, :, 0:u])),
                  f"cv{g}", writes=[bWBF[g]], nodeps=True)
        P.dma("sp", lambda e: e.dma_start(out=CST[:, :], in_=cst_d[:, :]), "cst", writes=[bCST])
        P.dma("sp", lambda e: e.dma_start(out=HSB[:, 0:2, :].rearrange("p a b -> p (a b)"), in_=wsp_d[:, :]), "wsp",
              writes=[bHSB[0], bHSB[1]])
        P.op("dve", lambda e: e.tensor_copy(out=WSPB[:, :], in_=HSB[:, 0:2, :].rearrange("p a b -> p (a b)")),
             reads=[bHSB[0], bHSB[1]], writes=[bWSPB])
        P.op("pool", lambda e: e.memset(ONES[:, :], 1.0), writes=[bONES])
        P.op("pool", lambda e: e.memset(E0[:, :], 1.0 / 128.0), writes=[bE0])
        P.op("pool", lambda e: e.memset(CAR[:, :, :], 0.0), writes=bCAR)

        seq = []
        for it in range(NT + 1):
            if it < NT:
                seq += list(range(0, P1_SLABS))
            if it >= 1:
                seq += list(range(P1_SLABS, NSLAB))
        ring_state = {"loaded": 0, "cur": 0}

        def ring_load(k):
            if k >= len(seq):
                return
            j = seq[k]
            slot = k % RING
            u = SLAB_USED[j]
            reads = [bWBF[cvgroup(j)]]
            P.dma("sp", (lambda e, j=j, slot=slot, u=u: e.dma_start(out=RINGT[slot][:, 0:u], in_=wbf[j, :, 0:u])),
                  f"rg{slot}", reads=reads, writes=[bSLOT[slot]])

        def ring_acquire(expect):
            k = ring_state["cur"]
            assert seq[k] == expect, (k, seq[k], expect)
            return RINGT[k % RING], bSLOT[k % RING]

        def ring_release():
            k = ring_state["cur"]
            ring_state["cur"] += 1
            ring_load(k + RING)


        def mm(out, lhsT, rhs, start, stop):
            f = lambda e: e.matmul(out, lhsT, rhs, start=start, stop=stop)
            f.ncols = out.shape[-1]
            return f

        def stat_mm(bank, width, sqr, first, last, halves=None):
            if halves is None:
                fns = [mm(PS[:, bank, 0:width], ONES[:, :], SQ[sqr][:, 0:width], first, last)]
                P.group(fns, reads=[bSQ[sqr], bONES], writes=[bBK[bank]])
            else:
                hw = halves
                for h in range(2):
                    fns = [mm(PS[:, bank + h, 0:hw], ONES[:, :], SQ[sqr][:, h * hw:(h + 1) * hw], first, last)]
                    P.group(fns, reads=[bSQ[sqr], bONES], writes=[bBK[bank + h]])

        def rstd_from_bank(bank, width, dst, bdst):
            P.op("act", lambda e: e.activation(out=SQT[:, 0:width], in_=PS[:, bank, 0:width], func=AF.Sqrt,
                                               scale=1.0 / D, bias=EPS),
                 reads=[bBK[bank]], writes=[bSQT])
            P.op("dve", lambda e: e.reciprocal(out=dst[:, 0:width], in_=SQT[:, 0:width]), reads=[bSQT], writes=[bdst])

        def load_x(t):
            TW = tile_T[t] + 16
            P.dma("sp", lambda e: e.dma_start(out=XR[:, :, 0:TW], in_=xin[t].rearrange("p (c w) -> p c w", c=8)),
                  "xr", writes=bXR)
            i = t % 2
            P.dma("sp", lambda e: e.dma_start(out=ICE[i][:, :], in_=ice_d[:, t * 64:(t + 1) * 64]), f"ice{i}",
                  writes=[bICE[i]])

        def n1_stat_chunk(t, c):
            TW = tile_T[t] + 16
            r = rnext("sqn", 3)
            P.op("act", lambda e, c=c, r=r: e.activation(out=SQN[r][:, 0:TW], in_=XR[:, c, 0:TW], func=AF.Square),
                 reads=[bXR[c]], writes=[bSQN[r]])
            return r

        def n1_stat_mm(t, c, r):
            TW = tile_T[t] + 16; HW = TW // 2
            for h in range(2):
                fns = [mm(PS[:, h, 0:HW], ONES[:, :], SQN[r][:, h * HW:(h + 1) * HW], c == 0, c == 7)]
                P.group(fns, reads=[bSQN[r], bONES], writes=[bBK[h]])

        def n1_rstd(t):
            T = tile_T[t]; TW = T + 16; HW = TW // 2
            P.op("act", lambda e: e.activation(out=SQT[:, 0:TW].rearrange("p (h w) -> p h w", h=2),
                                               in_=PS[:, 0:2, 0:HW], func=AF.Sqrt, scale=1.0 / D, bias=EPS),
                 reads=[bBK[0], bBK[1]], writes=[bSQT])
            P.op("dve", lambda e: e.reciprocal(out=RS1[:, 0:TW], in_=SQT[:, 0:TW]), reads=[bSQT], writes=[bRS1])

        def n1_vcols(t):
            T = tile_T[t]
            b = next_bank()
            nj = T // 128
            for j in range(nj):
                P.group([mm(PS[:, b, j:j + 1], RS1[:, 8 + 128 * j: 8 + 128 * (j + 1)], E0[:, 0:1], True, True)],
                        reads=[bRS1, bE0], writes=[bBK[b]])
            P.op("dve", lambda e: e.tensor_copy(out=RSC[:, 0:nj], in_=PS[:, b, 0:nj]), reads=[bBK[b]], writes=[bRSC])

        def n1_xg(t):
            TW = tile_T[t] + 16
            for c in range(8):
                P.op("dve", lambda e, c=c: e.scalar_tensor_tensor(out=XG[:, c, 0:TW], in0=XR[:, c, 0:TW],
                                                                   scalar=G(0, c), in1=RS1[:, 0:TW],
                                                                   op0=ALU.mult, op1=ALU.mult),
                     reads=[bXR[c], bCST, bRS1], writes=[bXG[c]])

        def emit_n1(t):
            rs = [n1_stat_chunk(t, c) for c in range(3)]
            for c in range(8):
                n1_stat_mm(t, c, rs[c])
                if c + 3 < 8:
                    rs.append(n1_stat_chunk(t, c + 3))
            n1_rstd(t)
            n1_xg(t)

        def hobj(i):
            return [bHSB[i]] + ([bVN[i - 6], bVN[i - 4]] if i >= 6 else [])

        def out_stage(T, slabs_fn, nk, rhs_fn, rhs_bufs, gpost, gpre, res_in, b_res_in, xs, bxs, final=False):
            pending = None
            for i in range(8):
                lhs_fn, slotb, rel = slabs_fn(i)
                b = next_bank()
                fns = [mm(PS[:, b, 0:T], lhs_fn(kc), rhs_fn(kc), kc == 0, kc == nk - 1) for kc in range(nk)]
                P.group(fns, reads=[slotb], per=[[rhs_bufs[kc]] for kc in range(nk)], writes=[bBK[b]])
                if rel:
                    ring_release()
                if pending is not None:
                    stat_mm(0, T, pending[0], pending[1] == 0, False)
                r = rnext("sq", 3)
                P.op("act", lambda e, r=r, b=b: e.activation(out=SQ[r][:, 0:T], in_=PS[:, b, 0:T], func=AF.Square),
                     reads=[bBK[b]], writes=[bSQ[r]])
                P.op("act", lambda e, i=i, b=b: e.activation(out=HSB[:, i, 0:T], in_=PS[:, b, 0:T], func=AF.Identity,
                                                             scale=G(gpost, i)),
                     reads=[bBK[b], bCST], writes=hobj(i))
                pending = (r, i)
            stat_mm(0, T, pending[0], False, True)
            rstd_from_bank(0, T, R1, bR1)
            for c in range(8):
                P.op("dve", lambda e, c=c: e.tensor_tensor(out=HSB[:, c, 0:T], in0=HSB[:, c, 0:T], in1=R1[:, 0:T],
                                                           op=ALU.mult),
                     reads=hobj(c) + [bR1], writes=hobj(c))
                P.op("dve", lambda e, c=c: e.tensor_tensor(out=xs(c), in0=res_in(c), in1=HSB[:, c, 0:T], op=ALU.add),
                     reads=[b_res_in[c]] + hobj(c), writes=[bxs[c]])
                if not final:
                    P.op("act", lambda e, c=c: e.activation(out=XG[:, c, 8:8 + T], in_=xs(c), func=AF.Identity,
                                                            scale=G(gpre, c)),
                         reads=[bxs[c], bCST], writes=[bXG[c]])
                    r = rnext("sq", 3)
                    P.op("act", lambda e, c=c, r=r: e.activation(out=SQ[r][:, 0:T], in_=xs(c), func=AF.Square),
                         reads=[bxs[c]], writes=[bSQ[r]])
                    stat_mm(1, T, r, c == 0, c == 7)
            if not final:
                rstd_from_bank(1, T, R2, bR2)

        def ffn_stage(T, base, n1_next=None):
            n1r = {}
            pend = [None]

            def finish(p):
                f, bu, r = p
                P.op("dve", lambda e, r=r, bu=bu, f=f: e.tensor_tensor(out=Hc[f][:, 0:T], in0=PS[:, bu, 0:T],
                                                                        in1=FT[r][:, 0:T], op=ALU.mult),
                     reads=[bBK[bu], bFT[r]], writes=[bPG[f]])

            for j in range(11):
                slab, slotb = ring_acquire(base + j)
                s3 = slab[:, :].rearrange("p (k c) -> p k c", k=8)
                if n1_next is not None and j < 8:
                    n1r[j] = n1_stat_chunk(n1_next, j)
                for q in range(2):
                    f = 2 * j + q
                    bg = next_bank(); bu = next_bank()
                    fg = [mm(PS[:, bg, 0:T], s3[:, kc, q * 128:(q + 1) * 128], XG[:, kc, 8:8 + T], kc == 0, kc == 7)
                          for kc in range(8)]
                    P.group(fg, reads=[slotb], per=[[bXG[kc]] for kc in range(8)], writes=[bBK[bg]])
                    fu = [mm(PS[:, bu, 0:T], s3[:, kc, 256 + q * 128:256 + (q + 1) * 128], XG[:, kc, 8:8 + T],
                             kc == 0, kc == 7) for kc in range(8)]
                    P.group(fu, reads=[slotb], per=[[bXG[kc]] for kc in range(8)], writes=[bBK[bu]])
                    r = rnext("ft", 3)
                    P.op("dve", lambda e, r=r, bg=bg: e.tensor_tensor(out=FT[r][:, 0:T], in0=PS[:, bg, 0:T],
                                                                       in1=R2[:, 0:T], op=ALU.mult),
                         reads=[bBK[bg], bR2], writes=[bFT[r]])
                    P.op("act", lambda e, r=r: e.activation(out=FT[r][:, 0:T], in_=FT[r][:, 0:T], func=AF.Silu),
                         reads=[bFT[r]], writes=[bFT[r]])
                    P.op("pool", lambda e, r=r: e.tensor_tensor(out=FT[r][:, 0:T], in0=FT[r][:, 0:T], in1=R2[:, 0:T],
                                                                op=ALU.mult),
                         reads=[bFT[r], bR2], writes=[bFT[r]])
                    if pend[0] is not None:
                        finish(pend[0])
                    pend[0] = (f, bu, r)
                ring_release()
                if n1_next is not None and 1 <= j <= 8:
                    n1_stat_mm(n1_next, j - 1, n1r[j - 1])
            finish(pend[0])
            if n1_next is not None:
                n1_rstd(n1_next)
                n1_xg(n1_next)

        def down_slabs(base):
            def fn(i):
                slab, slotb = ring_acquire(base + i)
                return (lambda kc: slab[:, kc * 128:(kc + 1) * 128]), slotb, True
            return fn

        def o_slabs(base):
            st = {}

            def fn(i):
                if i % 4 == 0:
                    st["s"] = ring_acquire(base + i // 4)
                slab, slotb = st["s"]
                s3 = slab[:, :].rearrange("p (k c) -> p k c", k=8)
                col = (i % 4) * 128
                return (lambda kc: s3[:, kc, col:col + 128]), slotb, (i % 4 == 3)
            return fn

        def phase1(t):
            T = tile_T[t]; TW = T + 16; HW = TW // 2; s = t % 2
            nj = T // 128
            ic = ICE[t % 2]; bic = bICE[t % 2]
            slab, slotb = ring_acquire(0)
            s3 = slab[:, :].rearrange("p (k c) -> p k c", k=8)
            for g in range(4):
                b0 = (2 + 2 * g) % 8
                for h in range(2):
                    fns = [mm(PS[:, b0 + h, 0:HW], s3[:, kc, g * 128:(g + 1) * 128], XG[:, kc, h * HW:(h + 1) * HW],
                              kc == 0, kc == 7) for kc in range(8)]
                    P.group(fns, reads=[slotb], per=[[bXG[kc]] for kc in range(8)], writes=[bBK[b0 + h]])
                P.op("act", lambda e, g=g, b0=b0: e.activation(
                    out=Zt[:, g, 0:TW].rearrange("p (h w) -> p h w", h=2), in_=PS[:, b0:b0 + 2, 0:HW], func=AF.Copy),
                     reads=[bBK[b0], bBK[b0 + 1]], writes=bZ)
            ring_release()
            bank_rr[0] = 0
            P.op("pool", lambda e: e.tensor_tensor(out=Pt[:, :, 1:TW], in0=Zt[:, :, 1:TW], in1=Zt[:, :, 0:TW - 1],
                                                   op=ALU.add), reads=bZ, writes=bP)
            P.op("pool", lambda e: e.tensor_tensor(out=Qt[:, 1:4, 2:TW - 1], in0=Pt[:, 1:4, 1:TW - 2],
                                                   in1=Pt[:, 1:4, 3:TW], op=ALU.add), reads=bP, writes=bQ)
            P.op("pool", lambda e: e.tensor_tensor(out=Pt[:, 2:4, 4:TW - 3], in0=Qt[:, 2:4, 2:TW - 5],
                                                   in1=Qt[:, 2:4, 6:TW - 1], op=ALU.add), reads=bQ, writes=bP)
            P.op("pool", lambda e: e.tensor_tensor(out=Qt[:, 3, 8:TW - 7], in0=Pt[:, 3, 4:TW - 11],
                                                   in1=Pt[:, 3, 12:TW - 3], op=ALU.add), reads=bP, writes=bQ)
            for g in range(4):
                S = Pt if g % 2 == 0 else Qt
                bS = bP if g % 2 == 0 else bQ
                w = 2 ** (g + 1)
                P.op("dve", lambda e, g=g, S=S, w=w: e.scalar_tensor_tensor(
                    out=Rt[:, g, 0:T], in0=S[:, g, 8:8 + T], scalar=1.0 / w, in1=Zt[:, g, 8:8 + T],
                    op0=ALU.mult, op1=ALU.subtract), reads=bS + bZ, writes=[bR[g]])
            for g in range(4):
                S = Pt if g % 2 == 0 else Qt
                bS = bP if g % 2 == 0 else bQ
                for ed in range(2):
                    c0 = 0 if ed == 0 else T - 8
                    k = g * 2 + ed
                    P.op("pool", lambda e, g=g, S=S, c0=c0, ed=ed, k=k: e.tensor_tensor(
                        out=EDG[:, k * 8:k * 8 + 8], in0=S[:, g, 8 + c0:16 + c0],
                        in1=ic[:, g * 16 + ed * 8: g * 16 + ed * 8 + 8], op=ALU.mult),
                         reads=bS + [bic], writes=[bEDG[k]])
            for g in range(4):
                for ed in range(2):
                    c0 = 0 if ed == 0 else T - 8
                    k = g * 2 + ed
                    P.op("pool", lambda e, g=g, c0=c0, k=k: e.tensor_tensor(
                        out=Rt[:, g, c0:c0 + 8], in0=EDG[:, k * 8:k * 8 + 8], in1=Zt[:, g, 8 + c0:16 + c0],
                        op=ALU.subtract), reads=[bEDG[k]] + bZ, writes=[bR[g]])
            slab, slotb = ring_acquire(1)
            s3 = slab[:, :].rearrange("p (k c) -> p k c", k=8)
            for jc in range(nj):
                b = next_bank()
                r = jc % 2
                fns = [mm(PS[:, b, 0:512], XG[:, kc, 8 + 128 * jc: 8 + 128 * (jc + 1)], s3[:, kc, 0:512],
                          kc == 0, kc == 7) for kc in range(8)]
                P.group(fns, reads=[slotb], per=[[bXG[kc]] for kc in range(8)], writes=[bBK[b]])
                VT = HSB[:, 4 + r, :]
                P.op("act", lambda e, b=b, jc=jc, VT=VT: e.activation(out=VT, in_=PS[:, b, 0:512],
                                                                       func=AF.Gelu_apprx_tanh),
                     reads=[bBK[b]], writes=[bVT[r]])
                for h in range(4):
                    P.op("dve", lambda e, h=h, r=r, VT=VT: e.bn_stats(out=ST[r][:, h, :], in_=VT[:, h * 128:(h + 1) * 128]),
                         reads=[bVT[r]], writes=[bVS[r]])
                    P.op("dve", lambda e, h=h, r=r: e.bn_aggr(out=MV[r][:, h, :], in_=ST[r][:, h, :]),
                         reads=[bVS[r]], writes=[bVS[r]])
                P.op("act", lambda e, r=r: e.activation(out=SV[r][:, :], in_=MV[r][:, :, 1], func=AF.Sqrt, bias=EPS),
                     reads=[bVS[r]], writes=[bVS[r]])
                P.op("dve", lambda e, r=r: e.reciprocal(out=RV[r][:, :], in_=SV[r][:, :]), reads=[bVS[r]],
                     writes=[bVS[r]])
                for h in range(4):
                    P.op("dve", lambda e, h=h, r=r, VT=VT: e.tensor_scalar(
                        out=VT[:, h * 128:(h + 1) * 128], in0=VT[:, h * 128:(h + 1) * 128],
                        scalar1=MV[r][:, h, 0:1], scalar2=RV[r][:, h:h + 1], op0=ALU.subtract, op1=ALU.mult),
                         reads=[bVT[r], bVS[r]], writes=[bVT[r]])
                P.op("pool", lambda e, jc=jc, VT=VT: e.tensor_tensor(out=VN[jc], in0=VT, in1=GVR, op=ALU.mult),
                     reads=[bVT[r], bCST], writes=[bVN[jc]])
            ring_release()
            slab, slotb = ring_acquire(2)
            s3 = slab[:, :].rearrange("p (k c) -> p k c", k=8)
            for j in range(4):
                b = next_bank()
                fns = [mm(PS[:, b, 0:T], s3[:, kc, j * 128:(j + 1) * 128], XG[:, kc, 8:8 + T], kc == 0, kc == 7)
                       for kc in range(8)]
                P.group(fns, reads=[slotb], per=[[bXG[kc]] for kc in range(8)], writes=[bBK[b]])
                P.op("act", lambda e, j=j, b=b: e.activation(out=HSB[:, j, 0:T], in_=PS[:, b, 0:T],
                                                             func=AF.Gelu_apprx_tanh), reads=[bBK[b]], writes=[bU[j]])
            ring_release()
            for g in range(4):
                b = next_bank()
                P.group([mm(PS[:, b, 0:T], WP[:, g, :], Rt[:, g, 0:T], True, True)], reads=[bWSPB, bR[g]],
                        writes=[bBK[b]])
                P.op("act", lambda e, g=g, b=b: e.activation(out=AB[:, 4 + g, 0:T], in_=PS[:, b, 0:T], func=AF.Identity,
                                                             scale=CST[:, C_PSC + g:C_PSC + g + 1]),
                     reads=[bBK[b], bCST], writes=[bAB[4 + g]])
            for jc in range(nj):
                b2 = next_bank()
                for h in range(4):
                    P.group([mm(PS[:, b2, h * 128:(h + 1) * 128], VN[jc][:, h * 128:(h + 1) * 128], WS[:, h, :],
                                True, True)], reads=[bVN[jc], bWSPB], writes=[bBK[b2]])
                rf = rnext("ft", 3)
                P.op("dve", lambda e, b2=b2, rf=rf: e.tensor_tensor(
                    out=FT[rf][:, :].rearrange("p (h q) -> p h q", h=4),
                    in0=PS[:, b2, :].rearrange("p (h q) -> p h q", h=4), in1=BS3, op=ALU.add),
                     reads=[bBK[b2], bCST], writes=[bFT[rf]])
                P.op("dve", lambda e, rf=rf, jc=jc: e.tensor_tensor(
                    out=AB[:, 0:4, jc * 128:(jc + 1) * 128], in0=FT[rf][:, :].rearrange("p (h q) -> p h q", h=4),
                    in1=HSB[:, 0:4, jc * 128:(jc + 1) * 128], op=ALU.mult),
                     reads=[bFT[rf]] + bU, writes=bAB[0:4])
            out_stage(T, o_slabs(3), 8, lambda kc: AB[:, kc, 0:T], bAB, gpost=1, gpre=2,
                      res_in=lambda c: XR[:, c, 8:8 + T], b_res_in=bXR, xs=lambda c: X[s][:, c, 0:T], bxs=bX[s])
            if t + 1 < NT:
                load_x(t + 1)
            ffn_stage(T, 5)
            out_stage(T, down_slabs(16), NFF, lambda kc: Hc[kc][:, 0:T], bPG[0:NFF], gpost=3, gpre=4,
                      res_in=lambda c: X[s][:, c, 0:T], b_res_in=bX[s], xs=lambda c: X[s][:, c, 0:T], bxs=bX[s])
            P.op("pool", lambda e: e.tensor_tensor(out=R2SQ[:, 0:T], in0=R2[:, 0:T], in1=R2[:, 0:T], op=ALU.mult),
                 reads=[bR2], writes=[bR2SQ])
            has_left = t not in SEG_START_L
            has_right = t not in SEG_END_L
            pend_cb = []
            for c in range(8):
                slab, slotb = ring_acquire(24 + c)
                s3 = slab[:, 0:NKC * 384].rearrange("p (k c) -> p k c", k=8)
                banks = [next_bank() for _ in range(3)]
                for w in range(3):
                    fns = [mm(PS[:, banks[w], 0:T], s3[:, kc, w * 128:(w + 1) * 128], XG[:, kc, 8:8 + T],
                              kc == 0, kc == 7) for kc in range(8)]
                    P.group(fns, reads=[slotb], per=[[bXG[kc]] for kc in range(8)], writes=[bBK[banks[w]]])
                ring_release()
                r = rnext("l1", 2)
                bB, bC, bV = banks
                P.op("dve", lambda e, r=r, bC=bC: e.tensor_tensor(out=GC[r][:, 0:T], in0=PS[:, bC, 0:T], in1=R2SQ[:, 0:T],
                                                                   op=ALU.mult),
                     reads=[bBK[bC], bR2SQ], writes=[bGC[r]])
                P.op("dve", lambda e, r=r, bB=bB: e.tensor_tensor(out=GBR[r][:, 0:T], in0=PS[:, bB, 0:T],
                                                                   in1=R2[:, 0:T], op=ALU.mult),
                     reads=[bBK[bB], bR2], writes=[bGBR[r]])
                P.op("dve", lambda e, r=r, bV=bV: e.tensor_tensor(out=TTt[r][:, 1:T + 1], in0=PS[:, bV, 0:T],
                                                                   in1=GC[r][:, 0:T], op=ALU.mult),
                     reads=[bBK[bV], bGC[r]], writes=[bTT[r]])
                if has_left:
                    sp_ = (t - 1) % 2
                    Tp = tile_T[t - 1]
                    P.op("pool", lambda e, c=c, r=r: e.tensor_scalar(out=CAR[:, c, 3:4], in0=TTt[r][:, 1:2],
                                                                      scalar1=CWc(2, c), scalar2=None, op0=ALU.mult),
                         reads=[bTT[r], bCST], writes=[bCAR[c]])
                    P.op("pool", lambda e, c=c: e.tensor_tensor(out=CAR[:, c, 3:4], in0=CAR[:, c, 3:4],
                                                                in1=CAR[:, c, 1:2], op=ALU.add),
                         reads=[bCAR[c]], writes=[bCAR[c]])
                    P.op("pool", lambda e, c=c, sp_=sp_, Tp=Tp: e.tensor_tensor(
                        out=CB[sp_][:, c, Tp - 1:Tp], in0=CAR[:, c, 3:4], in1=CAR[:, c, 2:3], op=ALU.mult),
                         reads=[bCAR[c]], writes=[bCB[sp_][c]])
                    P.op("pool", lambda e, c=c, r=r: e.tensor_copy(out=TTt[r][:, 0:1], in_=CAR[:, c, 0:1]),
                         reads=[bCAR[c]], writes=[bTT[r]])
                else:
                    P.op("pool", lambda e, r=r: e.memset(TTt[r][:, 0:1], 0.0), writes=[bTT[r]])
                while len(pend_cb) > 0:
                    pend_cb.pop(0)()
                P.op("dve", lambda e, c=c, r=r: e.tensor_scalar(out=CV[r][:, 0:T - 1], in0=TTt[r][:, 0:T - 1],
                                                                 scalar1=CWc(0, c), scalar2=None, op0=ALU.mult),
                     reads=[bTT[r], bCST], writes=[bCV[r]])
                P.op("dve", lambda e, c=c, r=r: e.scalar_tensor_tensor(out=CV[r][:, 0:T - 1], in0=TTt[r][:, 1:T],
                                                                        scalar=CWc(1, c), in1=CV[r][:, 0:T - 1],
                                                                        op0=ALU.mult, op1=ALU.add),
                     reads=[bTT[r], bCV[r], bCST], writes=[bCV[r]])
                P.op("dve", lambda e, c=c, r=r: e.scalar_tensor_tensor(out=CV[r][:, 0:T - 1], in0=TTt[r][:, 2:T + 1],
                                                                        scalar=CWc(2, c), in1=CV[r][:, 0:T - 1],
                                                                        op0=ALU.mult, op1=ALU.add),
                     reads=[bTT[r], bCV[r], bCST], writes=[bCV[r]])
                def cbmul(c=c, r=r):
                    P.op("pool", lambda e: e.tensor_tensor(out=CB[s][:, c, 0:T - 1], in0=GBR[r][:, 0:T - 1],
                                                           in1=CV[r][:, 0:T - 1], op=ALU.mult),
                         reads=[bGBR[r], bCV[r]], writes=[bCB[s][c]])
                pend_cb.append(cbmul)
                P.op("act", lambda e, c=c, r=r: e.activation(out=CAR[:, c, 0:1], in_=TTt[r][:, T:T + 1], func=AF.Copy),
                     reads=[bTT[r]], writes=[bCAR[c]])
                P.op("pool", lambda e, c=c, r=r: e.tensor_scalar(out=CAR[:, c, 1:2], in0=TTt[r][:, T - 1:T],
                                                                  scalar1=CWc(0, c), scalar2=None, op0=ALU.mult),
                     reads=[bTT[r], bCST], writes=[bCAR[c]])
                P.op("pool", lambda e, c=c, r=r: e.tensor_scalar(out=CAR[:, c, 3:4], in0=TTt[r][:, T:T + 1],
                                                                  scalar1=CWc(1, c), scalar2=None, op0=ALU.mult),
                     reads=[bTT[r], bCST], writes=[bCAR[c]])
                P.op("pool", lambda e, c=c: e.tensor_tensor(out=CAR[:, c, 1:2], in0=CAR[:, c, 1:2],
                                                            in1=CAR[:, c, 3:4], op=ALU.add),
                     reads=[bCAR[c]], writes=[bCAR[c]])
                P.op("act", lambda e, c=c, r=r: e.activation(out=CAR[:, c, 2:3], in_=GBR[r][:, T - 1:T], func=AF.Copy),
                     reads=[bGBR[r]], writes=[bCAR[c]])
                if not has_right:
                    P.op("pool", lambda e, c=c: e.tensor_tensor(out=CB[s][:, c, T - 1:T], in0=CAR[:, c, 1:2],
                                                                in1=CAR[:, c, 2:3], op=ALU.mult),
                         reads=[bCAR[c]], writes=[bCB[s][c]])
            while len(pend_cb) > 0:
                pend_cb.pop(0)()

        def phase2(t, n1_next):
            T = tile_T[t]; s = t % 2
            out_stage(T, o_slabs(32), 8, lambda kc: CB[s][:, kc, 0:T], bCB[s], gpost=5, gpre=6,
                      res_in=lambda c: X[s][:, c, 0:T], b_res_in=bX[s], xs=lambda c: X[s][:, c, 0:T], bxs=bX[s])
            ffn_stage(T, 34, n1_next)
            out_stage(T, down_slabs(45), NFF, lambda kc: Hc[kc][:, 0:T], bPG[0:NFF], gpost=7, gpre=None,
                      res_in=lambda c: X[s][:, c, 0:T], b_res_in=bX[s], xs=lambda c: X[s][:, c, 0:T], bxs=bX[s],
                      final=True)
            P.dma("sp", lambda e: e.dma_start(out=yout[t].rearrange("p (c w) -> p c w", c=8), in_=X[s][:, :, 0:T]),
                  f"y{s}", reads=bX[s])

        SEG_START_L = [i for i in SEG_START if i < NT]
        SEG_END_L = [i for i in SEG_END if i < NT] + [NT - 1]

        for k in range(RING):
            ring_load(k)
        load_x(0)
        emit_n1(0)
        for it in range(NT + 1):
            if it < NT:
                phase1(it)
                if it == 0 and NT > 1:
                    emit_n1(1)
            if it >= 1:
                nxt = it + 1 if (it + 1 < NT) else None
                phase2(it - 1, nxt)
        for s in range(2):
            if P.dcnt.get(f"y{s}", 0):
                P.final_wait("sp", f"y{s}", P.dcnt[f"y{s}"])

        engmap = {"pe": "tensor", "act": "scalar", "dve": "vector", "pool": "gpsimd", "sp": "sync"}
        with nc.Block() as block:
            def make(engname):
                items = P.lists[engname]

                def body(e):
                    for it_ in items:
                        if it_[0] == "wait":
                            e.wait_ge(sems[it_[1]], it_[2])
                        else:
                            ins = it_[1](e)
                            if it_[2] is not None:
                                ins.then_inc(sems[it_[2]], it_[3])
                return body
            for en, attr in engmap.items():
                getattr(block, attr)(make(en))
        build_nc.stats = {k: len(v) for k, v in P.lists.items()}
        build_nc.P = P
    return nc


_NC_CACHE = {}


def prepare_core_inputs(core, xall, tile_T, shared):
    tiles, _ = core_plan(core)
    m = dict(shared)
    NT = len(tile_T)
    ice = np.zeros((128, NT * 64), np.float32)
    for t in range(NT):
        sq, off, T = tiles[t]
        TW = T + 16
        lo, hi = off - 8, off + T + 8
        buf = np.zeros((TW, D), np.float32)
        a, b = max(lo, 0), min(hi, SEQ)
        buf[a - lo:b - lo] = xall[sq, a:b]
        m[f"xin{t}"] = np.ascontiguousarray(buf.reshape(TW, 8, 128).transpose(2, 1, 0)).reshape(128, 8 * TW)
        for g in range(4):
            h = 2 ** g
            for ed in range(2):
                pos = off + (np.arange(8) if ed == 0 else T - 8 + np.arange(8))
                cnt = np.minimum(pos + h, SEQ) - np.maximum(pos - h, 0)
                ice[:, t * 64 + g * 16 + ed * 8: t * 64 + g * 16 + ed * 8 + 8] = (1.0 / cnt).astype(np.float32)
    m["ice"] = ice
    return m


def run_cores(inputs, tile_T, n_cores=N_CORES, trace=False):
    key = tuple(tile_T)
    if key not in _NC_CACHE:
        _NC_CACHE[key] = build_nc(list(tile_T))
    nc = _NC_CACHE[key]
    xall = np.concatenate([np.asarray(inputs["x_prompt"], np.float32), np.asarray(inputs["x_sample"], np.float32)], 0)
    inp = {k: np.asarray(v, np.float32) for k, v in inputs.items() if not k.startswith("x_")}
    shared = {"wsrc": build_wsrc(inp), "cst": build_cst(inp), "wsp": build_wsp(inp)}
    in_maps = [prepare_core_inputs(c, xall, tile_T, shared) for c in range(n_cores)]
    res = run_bass_kernel_spmd(nc, in_maps, core_ids=list(range(n_cores)), trace=trace)
    return res, xall


def kernel(**inputs):
    res, xall = run_cores(inputs, TILE_T)
    yall = np.zeros((12, SEQ, D), np.float32)
    for core in range(N_CORES):
        tiles, base = core_plan(core)
        r = res.results[core]
        for t, (sq, off, T) in enumerate(tiles):
            if t >= 8:
                if core % 2 == 0 and off >= 2048:
                    continue
                if core % 2 == 1 and off < 2048:
                    continue
            y = np.asarray(r[f"y{t}"]).reshape(128, 8, T)
            yall[sq, off:off + T] = y.transpose(2, 1, 0).reshape(T, D)
    return yall[:8].copy(), yall[8:].copy()
```
